# Optimizing a Trainium2 kernel written in Bass

```python
import math
import jax, jax.numpy as jnp
from jax import lax
import numpy as np

D_MODEL = 1024
BATCH = 16
SEQ = 4096
DEPTH = 1

CHUNK = 64
Q_BLOCK = 128
TOKEN_BLOCK = 128
EPS = 1e-6

SB_HEADS = 8
SB_HEAD_DIM = 64
SB_WIDTH = SB_HEADS * SB_HEAD_DIM

MLA_HEADS = 8
MLA_NOPE_DIM = 64
MLA_ROPE_DIM = 32
MLA_QK_DIM = MLA_NOPE_DIM + MLA_ROPE_DIM
MLA_V_DIM = 64
MLA_Q_RANK = 384
MLA_KV_RANK = 256
MLA_WIDTH = MLA_HEADS * MLA_V_DIM
ROPE_THETA = 10000.0

MIX_WIDTH = SB_WIDTH + MLA_WIDTH
IN_SPLITS = (SB_WIDTH, 2 * SB_WIDTH, 3 * SB_WIDTH,
             3 * SB_WIDTH + MLA_Q_RANK,
             3 * SB_WIDTH + MLA_Q_RANK + MLA_KV_RANK)
IN_PROJ_WIDTH = 3 * SB_WIDTH + MLA_Q_RANK + MLA_KV_RANK + MLA_ROPE_DIM

PEER_HEADS = 8
PEER_N_KEYS = 128
PEER_N_EXPERTS = PEER_N_KEYS * PEER_N_KEYS
PEER_KEY_DIM = 256
PEER_HALF_DIM = PEER_KEY_DIM // 2
PEER_TOPK = 16

kernel_name = "hybrid_stickbreak_mla_peer_block"


def rms_norm(x, g):
    xf = x.astype(jnp.float32)
    y = xf * lax.rsqrt(jnp.mean(xf * xf, axis=-1, keepdims=True) + EPS)
    return (y * g.astype(jnp.float32)).astype(x.dtype)


def rope(x, pos):
    r = x.shape[-1]
    half = r // 2
    inv_freq = 1.0 / (ROPE_THETA ** (jnp.arange(half, dtype=jnp.float32) * (2.0 / r)))
    ang = pos.astype(jnp.float32)[..., None] * inv_freq
    cos = jnp.cos(ang)[:, :, None, :]
    sin = jnp.sin(ang)[:, :, None, :]
    xf = x.astype(jnp.float32)
    x1, x2 = xf[..., :half], xf[..., half:]
    out = jnp.concatenate([x1 * cos - x2 * sin, x1 * sin + x2 * cos], axis=-1)
    return out.astype(x.dtype)


def to_query_blocks(q):
    b, s, h, d = q.shape
    return q.reshape(b, s // Q_BLOCK, Q_BLOCK, h, d).transpose(1, 0, 3, 2, 4)


def from_query_blocks(o):
    nb, b, qb, h, d = o.shape
    return o.transpose(1, 0, 2, 3, 4).reshape(b, nb * qb, h * d)


def stick_breaking_attention(q, k, v):
    b, s, h, d = q.shape
    nb = s // Q_BLOCK
    scale = d ** -0.5
    kpos = jnp.arange(s)

    def block(args):
        qi, bi = args
        z = jnp.einsum('bhqd,bshd->bhqs', qi, k,
                       preferred_element_type=jnp.float32) * scale
        qpos = bi * Q_BLOCK + jnp.arange(Q_BLOCK)
        strict = kpos[None, :] < qpos[:, None]
        log_fail = jnp.where(strict, jax.nn.log_sigmoid(-z), 0.0)
        later = lax.cumsum(log_fail, axis=3, reverse=True) - log_fail
        a = jnp.where(strict, jnp.exp(jax.nn.log_sigmoid(z) + later), 0.0)
        return jnp.einsum('bhqs,bshd->bqhd', a.astype(v.dtype), v)

    out = lax.map(block, (to_query_blocks(q), jnp.arange(nb)))
    return from_query_blocks(out)


def chunk_causal_softmax_attention(q, k, v):
    b, s, h, dq = q.shape
    nb = s // Q_BLOCK
    scale = dq ** -0.5
    kchunk = jnp.arange(s) // CHUNK

    def block(args):
        qi, bi = args
        sc = jnp.einsum('bhqd,bshd->bhqs', qi, k,
                        preferred_element_type=jnp.float32) * scale
        qchunk = (bi * Q_BLOCK + jnp.arange(Q_BLOCK)) // CHUNK
        allowed = kchunk[None, :] <= qchunk[:, None]
        p = jax.nn.softmax(jnp.where(allowed, sc, -jnp.inf), axis=-1)
        return jnp.einsum('bhqs,bshd->bqhd', p.astype(v.dtype), v)

    out = lax.map(block, (to_query_blocks(q), jnp.arange(nb)))
    return from_query_blocks(out)


def peer_ffn(h, w_q, sub_keys, u, v):
    b, s, d = h.shape
    nb = (b * s) // TOKEN_BLOCK
    k = PEER_TOPK

    def block(xt):
        q = (xt @ w_q).reshape(TOKEN_BLOCK, PEER_HEADS, 2, PEER_HALF_DIM)
        sc = jnp.einsum('thcd,chnd->thcn', q, sub_keys,
                        preferred_element_type=jnp.float32)
        top_s, top_i = lax.top_k(sc, k)
        cand_s = top_s[:, :, 0, :, None] + top_s[:, :, 1, None, :]
        cand_i = top_i[:, :, 0, :, None] * PEER_N_KEYS + top_i[:, :, 1, None, :]
        best_s, best_pos = lax.top_k(cand_s.reshape(TOKEN_BLOCK, PEER_HEADS, k * k), k)
        idx = jnp.take_along_axis(cand_i.reshape(TOKEN_BLOCK, PEER_HEADS, k * k),
                                  best_pos, axis=-1)
        g = jax.nn.softmax(best_s, axis=-1)
        pre = jnp.einsum('td,thkd->thk', xt, u[idx], preferred_element_type=jnp.float32)
        coef = (g * jax.nn.gelu(pre, approximate=False)).astype(v.dtype)
        return jnp.einsum('thk,thkd->td', coef, v[idx])

    out = lax.map(block, h.reshape(nb, TOKEN_BLOCK, d))
    return out.reshape(b, s, d)


def setup_inputs(seed: int = 0) -> dict:
    key = jax.random.key(seed)
    ks = jax.random.split(key, 20)
    f32 = jnp.float32

    def w(k_, shape, fan_in):
        return jax.random.normal(k_, shape, f32) * (fan_in ** -0.5)

    def gain(k_, shape):
        return 1.0 + 0.02 * jax.random.normal(k_, shape, f32)

    x = jax.random.normal(ks[0], (BATCH, SEQ, D_MODEL), f32)
    offset = jax.random.randint(ks[1], (BATCH, 1), 0, 1024, dtype=jnp.int32)
    positions = (offset + jnp.arange(SEQ, dtype=jnp.int32)[None, :]).astype(jnp.int32)
    return {
        "x": x,
        "positions": positions,
        "attn_norm": gain(ks[2], (DEPTH, D_MODEL)),
        "w_in": w(ks[3], (DEPTH, D_MODEL, IN_PROJ_WIDTH), D_MODEL),
        "cq_norm": gain(ks[4], (DEPTH, MLA_Q_RANK)),
        "w_uq": w(ks[5], (DEPTH, MLA_Q_RANK, MLA_HEADS * MLA_QK_DIM), MLA_Q_RANK),
        "ckv_norm": gain(ks[6], (DEPTH, MLA_KV_RANK)),
        "w_ukv": w(ks[7], (DEPTH, MLA_KV_RANK, MLA_HEADS * (MLA_NOPE_DIM + MLA_V_DIM)), MLA_KV_RANK),
        "q_norm": gain(ks[8], (DEPTH, MLA_QK_DIM)),
        "k_norm": gain(ks[9], (DEPTH, MLA_QK_DIM)),
        "sb_out_norm": gain(ks[10], (DEPTH, SB_WIDTH)),
        "mla_out_norm": gain(ks[11], (DEPTH, MLA_WIDTH)),
        "w_o": w(ks[12], (DEPTH, MIX_WIDTH, D_MODEL), MIX_WIDTH),
        "ffn_norm": gain(ks[13], (DEPTH, D_MODEL)),
        "peer_w_q": w(ks[14], (DEPTH, D_MODEL, PEER_HEADS * PEER_KEY_DIM), D_MODEL),
        "peer_sub_keys": w(ks[15], (DEPTH, 2, PEER_HEADS, PEER_N_KEYS, PEER_HALF_DIM), PEER_HALF_DIM),
        "peer_u": w(ks[16], (DEPTH, PEER_N_EXPERTS, D_MODEL), D_MODEL),
        "peer_v": w(ks[17], (DEPTH, PEER_N_EXPERTS, D_MODEL), D_MODEL),
    }


def reference(x, positions, attn_norm, w_in, cq_norm, w_uq, ckv_norm, w_ukv, q_norm, k_norm,
              sb_out_norm, mla_out_norm, w_o, ffn_norm, peer_w_q, peer_sub_keys, peer_u, peer_v):
    b, s, _ = x.shape
    for l in range(DEPTH):
        h = rms_norm(x, attn_norm[l])
        proj = h @ w_in[l]
        sb_q, sb_k, sb_v, cq, ckv, k_pe = jnp.split(proj, IN_SPLITS, axis=-1)

        sb = stick_breaking_attention(sb_q.reshape(b, s, SB_HEADS, SB_HEAD_DIM),
                                      sb_k.reshape(b, s, SB_HEADS, SB_HEAD_DIM),
                                      sb_v.reshape(b, s, SB_HEADS, SB_HEAD_DIM))

        q = (rms_norm(cq, cq_norm[l]) @ w_uq[l]).reshape(b, s, MLA_HEADS, MLA_QK_DIM)
        kv = (rms_norm(ckv, ckv_norm[l]) @ w_ukv[l]).reshape(b, s, MLA_HEADS, MLA_NOPE_DIM + MLA_V_DIM)
        k_nope, v = kv[..., :MLA_NOPE_DIM], kv[..., MLA_NOPE_DIM:]
        k_rot = jnp.broadcast_to(k_pe[:, :, None, :], (b, s, MLA_HEADS, MLA_ROPE_DIM))
        k = jnp.concatenate([k_nope, k_rot], axis=-1)
        q = rms_norm(q, q_norm[l])
        k = rms_norm(k, k_norm[l])
        q = jnp.concatenate([q[..., :MLA_NOPE_DIM], rope(q[..., MLA_NOPE_DIM:], positions)], axis=-1)
        k = jnp.concatenate([k[..., :MLA_NOPE_DIM], rope(k[..., MLA_NOPE_DIM:], positions)], axis=-1)
        mla = chunk_causal_softmax_attention(q, k, v)

        mixed = jnp.concatenate([rms_norm(sb, sb_out_norm[l]),
                                 rms_norm(mla, mla_out_norm[l])], axis=-1)
        x = x + mixed @ w_o[l]

        x = x + peer_ffn(rms_norm(x, ffn_norm[l]), peer_w_q[l], peer_sub_keys[l],
                         peer_u[l], peer_v[l])
    return x
```

```python
import math
from contextlib import ExitStack

import numpy as np
import concourse.bass as bass
import concourse.mybir as mybir
from concourse.bass_utils import run_bass_kernel_spmd

F32 = mybir.dt.float32
BF16 = mybir.dt.bfloat16
I32 = mybir.dt.int32
U32 = mybir.dt.uint32
AF = mybir.ActivationFunctionType
ALU = mybir.AluOpType
AX = mybir.AxisListType

ENGS = ["pe", "act", "dve", "pool", "sp"]
NSEQ = 2
S_LEN = 4096
NB = 32
D = 1024
EPS = 1e-6
BIG = 3000.0


class Buf:
    __slots__ = ("name", "w", "readers")

    def __init__(self, name):
        self.name = name
        self.w = None
        self.readers = []


class Sched:
    def __init__(self, nc, stack):
        self.nc = nc
        self.stack = stack
        self.q = {e: [] for e in ENGS}
        self.cnt = {e: 0 for e in ENGS}
        self.sem = {e: stack.enter_context(nc.semaphore("s_" + e)) for e in ENGS}
        self.waited = {e: {} for e in ENGS}
        self.dsem = {}
        self.dcnt = {}
        self.bufs = {}
        self.rec = None

    def record(self, thunk):
        self.rec = []
        try:
            thunk()
        finally:
            r = self.rec
            self.rec = None
        return r

    def emit(self, item):
        kind, a, fn, reads, writes, queue = item
        if kind == 0:
            self.op(a, fn, reads, writes)
        else:
            self.dma(a, fn, reads, writes, queue=queue)

    def emit_interleaved(self, la, lb):
        ia = ib = 0
        na, nb = len(la), len(lb)
        while ia < na or ib < nb:
            if ib >= nb or (ia < na and ia * nb <= ib * na):
                self.emit(la[ia])
                ia += 1
            else:
                self.emit(lb[ib])
                ib += 1

    def B(self, name):
        b = self.bufs.get(name)
        if b is None:
            b = self.bufs[name] = Buf(name)
        return b

    def _bl(self, xs):
        if isinstance(xs, str):
            return [self.B(xs)]
        return [self.B(x) for x in xs if x is not None]

    def _deps(self, eng, reads, writes):
        deps = {}
        for b in reads:
            if b.w is not None:
                k, v = b.w
                if deps.get(k, 0) < v:
                    deps[k] = v
        for b in writes:
            if b.w is not None:
                k, v = b.w
                if k != eng and deps.get(k, 0) < v:
                    deps[k] = v
            for (k, v) in b.readers:
                if deps.get(k, 0) < v:
                    deps[k] = v
        out = []
        wd = self.waited[eng]
        for k, v in deps.items():
            if k == eng and eng == "pe":
                continue
            if wd.get(k, 0) >= v:
                continue
            wd[k] = v
            out.append((k, v))
        return out

    def _mark(self, tok, reads, writes):
        for b in writes:
            b.w = tok
            b.readers = []
        for b in reads:
            if b.w is tok:
                continue
            rs = b.readers
            for i, (k, v) in enumerate(rs):
                if k == tok[0]:
                    rs[i] = tok
                    break
            else:
                rs.append(tok)

    def op(self, eng, fn, reads=(), writes=()):
        if self.rec is not None:
            self.rec.append((0, eng, fn, reads, writes, None))
            return None
        reads = self._bl(reads)
        writes = self._bl(writes)
        waits = self._deps(eng, reads, writes)
        self.cnt[eng] += 1
        tok = (eng, self.cnt[eng])
        self.q[eng].append((waits, fn, self.sem[eng], 1))
        self._mark(tok, reads, writes)
        return tok

    def dma(self, key, fn, reads=(), writes=(), queue="sp"):
        if self.rec is not None:
            self.rec.append((1, key, fn, reads, writes, queue))
            return None
        reads = self._bl(reads)
        writes = self._bl(writes)
        k = "d_" + key
        if k not in self.dsem:
            self.dsem[k] = self.stack.enter_context(self.nc.semaphore(k))
            self.dcnt[k] = 0
        waits = self._deps(queue, reads, writes)
        self.dcnt[k] += 16
        tok = (k, self.dcnt[k])
        self.q[queue].append((waits, fn, self.dsem[k], 16))
        self._mark(tok, reads, writes)
        return tok

    def barrier(self):
        for eng in ENGS:
            waits = []
            wd = self.waited[eng]
            for k, v in self.dcnt.items():
                if v and wd.get(k, 0) < v:
                    waits.append((k, v))
                    wd[k] = v
            for e in ENGS:
                if e == eng:
                    continue
                v = self.cnt[e]
                if v and wd.get(e, 0) < v:
                    waits.append((e, v))
                    wd[e] = v
            if waits:
                self.q[eng].append((waits, None, None, 0))

    def replay(self, block):
        s = self

        def semof(k):
            return s.sem[k] if k in s.sem else s.dsem[k]

        def run(name, eng):
            for waits, fn, sem, inc in s.q[name]:
                for k, v in waits:
                    eng.wait_ge(semof(k), v)
                if fn is not None:
                    fn(eng).then_inc(sem, inc)

        @block.tensor
        def _(e):
            run("pe", e)

        @block.scalar
        def _(e):
            run("act", e)

        @block.vector
        def _(e):
            run("dve", e)

        @block.gpsimd
        def _(e):
            run("pool", e)

        @block.sync
        def _(e):
            run("sp", e)


def build_program(do_peer=True):
    nc = bass.Bass("TRN2", target_bir_lowering=False)
    dt_in = lambda n, s, d=F32: nc.dram_tensor(n, list(s), d, kind="ExternalInput").ap()
    x_d = dt_in("x", [NSEQ, S_LEN, D])
    pos_d = dt_in("pos", [128, NSEQ * NB], I32)
    invf_d = dt_in("invf", [1, 16])
    g_attn_d = dt_in("g_attn", [1, 1024])
    w_in_d = dt_in("w_in", [1024, 2208])
    g_cq_d = dt_in("g_cq", [1, 384])
    w_uq_d = dt_in("w_uq", [384, 768])
    g_ckv_d = dt_in("g_ckv", [1, 256])
    w_ukv_d = dt_in("w_ukv", [256, 1024])
    g_q_d = dt_in("g_q", [1, 96])
    g_k_d = dt_in("g_k", [1, 96])
    g_sbo_d = dt_in("g_sbo", [1, 512])
    g_mlao_d = dt_in("g_mlao", [1, 512])
    w_o_d = dt_in("w_o", [1024, 1024])
    g_ffn_d = dt_in("g_ffn", [1, 1024])
    w_pq_d = dt_in("w_pq", [1024, 2048])
    skT_d = dt_in("skT", [128, 16, 128])
    uT_d = dt_in("uT", [1024, 16384])
    v_d = dt_in("pv", [16384, 1024])
    out_d = nc.dram_tensor("out", [NSEQ, S_LEN, D], F32, kind="ExternalOutput").ap()

    scr = lambda n, s, d: nc.dram_tensor(n, list(s), d, kind="Internal").ap()
    qt_sb_d = scr("qt_sb", [NSEQ, 8, 64, S_LEN], BF16)
    kt_sb_d = scr("kt_sb", [NSEQ, 8, 64, S_LEN], BF16)
    qt_ml_d = scr("qt_ml", [NSEQ, 8, 96, S_LEN], BF16)
    kt_ml_d = scr("kt_ml", [NSEQ, 8, 96, S_LEN], BF16)
    mixraw_d = scr("mixraw", [NSEQ, S_LEN, D], F32)
    u_bf_d = scr("u_bf", [128, 128, 1024], BF16)
    v_bf_d = scr("v_bf", [128, 128, 1024], BF16)

    with ExitStack() as st:
        S = Sched(nc, st)

        uid = [0]

        def sbt(stack, n, s, d):
            uid[0] += 1
            return stack.enter_context(nc.sbuf_tensor(f"t{uid[0]}_{n}", list(s), d))

        def MM(out, lhsT, rhs, start, stop, reads, writes):
            S.op("pe", lambda e: e.matmul(out, lhsT=lhsT, rhs=rhs, start=start, stop=stop,
                                          skip_group_check=True), reads, writes)

        def TR(out, in_, ident_ap, reads, writes):
            S.op("pe", lambda e: e.transpose(out=out, in_=in_, identity=ident_ap), reads, writes)

        def ACT(out, in_, func, reads, writes, **kw):
            S.op("act", lambda e: e.activation(out=out, in_=in_, func=func, **kw), reads, writes)

        def TT(eng, out, in0, in1, op, reads, writes):
            S.op(eng, lambda e: e.tensor_tensor(out=out, in0=in0, in1=in1, op=op), reads, writes)

        def TS(eng, out, in0, s1, op0, reads, writes, s2=None, op1=None):
            if op1 is None:
                S.op(eng, lambda e: e.tensor_scalar(out=out, in0=in0, scalar1=s1, scalar2=None, op0=op0),
                     reads, writes)
            else:
                S.op(eng, lambda e: e.tensor_scalar(out=out, in0=in0, scalar1=s1, scalar2=s2, op0=op0, op1=op1),
                     reads, writes)

        def STT(out, in0, scalar, in1, op0, op1, reads, writes):
            S.op("dve", lambda e: e.scalar_tensor_tensor(out=out, in0=in0, scalar=scalar, in1=in1,
                                                         op0=op0, op1=op1), reads, writes)

        def CP(eng, out, in_, reads, writes):
            if eng == "act":
                ACT(out, in_, AF.Copy, reads, writes)
            else:
                S.op(eng, lambda e: e.tensor_copy(out=out, in_=in_), reads, writes)

        def RED(out, in_, reads, writes, op=ALU.add):
            S.op("dve", lambda e: e.tensor_reduce(out=out, in_=in_, axis=AX.X, op=op), reads, writes)

        def RECIP(out, in_, reads, writes):
            S.op("dve", lambda e: e.reciprocal(out=out, in_=in_), reads, writes)

        def MSET(eng, ap, val, writes):
            S.op(eng, lambda e: e.memset(ap, val), (), writes)

        def DMA(key, out, in_, reads, writes, queue="sp"):
            S.dma(key, lambda e: e.dma_start(out=out, in_=in_), reads, writes, queue=queue)

        PS = [st.enter_context(nc.psum_tensor(f"ps{i}", [128, 512], F32)) for i in range(8)]
        PSB = [p[:].bitcast(BF16) for p in PS]
        PN = [f"ps{i}" for i in range(8)]

        identf = sbt(st, "identf", [128, 128], F32)
        ident = sbt(st, "ident", [128, 128], BF16)
        negm = sbt(st, "negm", [128, 128], BF16)
        negmf = sbt(st, "negmf", [128, 128], F32)
        zeros_bf = sbt(st, "zeros_bf", [128, 512], BF16)
        zero1 = sbt(st, "zero1", [128, 1], F32)
        epsc = sbt(st, "epsc", [128, 1], F32)
        g_ffn = sbt(st, "g_ffn", [128, 1024], F32)
        iota_i = sbt(st, "iota_i", [128, 128], I32)
        iota_f = sbt(st, "iota_f", [128, 128], F32)
        thr_i = sbt(st, "thr_i", [128, 15], I32)
        thr_f = sbt(st, "thr_f", [128, 15], F32)
        ast = ExitStack()
        g_attn = sbt(ast, "g_attn", [128, 1024], F32)
        g_cq = sbt(ast, "g_cq", [128, 384], F32)
        g_ckv = sbt(ast, "g_ckv", [128, 256], F32)
        g_q = sbt(ast, "g_q", [128, 96], F32)
        g_k = sbt(ast, "g_k", [128, 96], F32)
        g_sbo = sbt(ast, "g_sbo", [128, 512], F32)
        g_mlao = sbt(ast, "g_mlao", [128, 512], F32)
        cs = sbt(ast, "cs", [128, NSEQ * NB, 32], F32)
        vsb_all = sbt(ast, "vsb_all", [128, NB, 512], BF16)
        vaug_all = sbt(ast, "vaug_all", [128, NB, 8, 65], BF16)
        stg = [sbt(ast, f"stg{i}", [128, 1024], BF16) for i in range(4)]
        S.op("pool", lambda e: e.iota(iota_i[:], pattern=[[1, 128]], base=0, channel_multiplier=0), (), ["iota_i"])
        CP("dve", iota_f[:], iota_i[:], ["iota_i"], ["iota_f"])
        S.op("pool", lambda e: e.iota(thr_i[:], pattern=[[16, 15]], base=16, channel_multiplier=0), (), ["thr_i"])
        CP("dve", thr_f[:], thr_i[:], ["thr_i"], ["thr_f"])

        MSET("pool", identf[:], 0.0, ["identf"])
        S.op("pool", lambda e: e.affine_select(out=identf[:], in_=identf[:], pattern=[[-1, 128]],
                                               compare_op=ALU.not_equal, fill=1.0, base=0,
                                               channel_multiplier=1), ["identf"], ["identf"])
        CP("dve", ident[:], identf[:], ["identf"], ["ident"])
        MSET("pool", negmf[:], 0.0, ["negmf"])
        S.op("pool", lambda e: e.affine_select(out=negmf[:], in_=negmf[:], pattern=[[-1, 128]],
                                               compare_op=ALU.is_gt, fill=-BIG, base=0,
                                               channel_multiplier=1), ["negmf"], ["negmf"])
        CP("dve", negm[:], negmf[:], ["negmf"], ["negm"])
        MSET("dve", zeros_bf[:], 0.0, ["zeros_bf"])
        MSET("dve", zero1[:], 0.0, ["zero1"])
        MSET("dve", epsc[:], EPS, ["epsc"])
        MSET("pool", vaug_all[:, :, :, 64:65], 1.0, ["vaug_all"])
        for nm, t, src, n in [("g_attn", g_attn, g_attn_d, 1024), ("g_ffn", g_ffn, g_ffn_d, 1024),
                              ("g_cq", g_cq, g_cq_d, 384), ("g_ckv", g_ckv, g_ckv_d, 256),
                              ("g_q", g_q, g_q_d, 96), ("g_k", g_k, g_k_d, 96),
                              ("g_sbo", g_sbo, g_sbo_d, 512), ("g_mlao", g_mlao, g_mlao_d, 512)]:
            DMA(nm, t[:], src.broadcast_to([128, n]), [], [nm])

        with ExitStack() as ph:
            NJ = NSEQ * NB
            posi = sbt(ph, "posi", [128, NJ], I32)
            posf = sbt(ph, "posf", [128, NJ], F32)
            invf = sbt(ph, "invf", [128, 16], F32)
            ang = sbt(ph, "ang", [128, NJ, 16], F32)
            kk = sbt(ph, "kk", [128, NJ, 16], F32)
            kki = sbt(ph, "kki", [128, NJ, 16], I32)
            yy = sbt(ph, "yy", [128, NJ, 16], F32)
            msk = sbt(ph, "msk", [128, NJ, 16], F32)
            DMA("posi", posi[:], pos_d, [], ["posi"])
            DMA("invf", invf[:], invf_d.broadcast_to([128, 16]), [], ["invf"])
            CP("dve", posf[:], posi[:], ["posi"], ["posf"])
            TT("dve", ang[:], posf[:].unsqueeze(2).broadcast_to([128, NJ, 16]),
               invf[:].unsqueeze(1).broadcast_to([128, NJ, 16]), ALU.mult, ["posf", "invf"], ["ang"])
            C1 = 6.28125
            C2 = 2.0 * math.pi - C1
            for which, shift in ((1, 0.0), (0, math.pi / 2)):
                TS("dve", yy[:], ang[:], shift, ALU.add, ["ang"], ["yy"])
                TS("dve", kk[:], yy[:], 1.0 / (2.0 * math.pi), ALU.mult, ["yy"], ["kk"])
                CP("dve", kki[:], kk[:], ["kk"], ["kki"])
                CP("dve", kk[:], kki[:], ["kki"], ["kk"])
                STT(yy[:], kk[:], -C1, yy[:], ALU.mult, ALU.add, ["kk", "yy"], ["yy"])
                STT(yy[:], kk[:], -C2, yy[:], ALU.mult, ALU.add, ["kk", "yy"], ["yy"])
                TS("dve", msk[:], yy[:], math.pi, ALU.is_gt, ["yy"], ["msk"])
                STT(yy[:], msk[:], -2.0 * math.pi, yy[:], ALU.mult, ALU.add, ["msk", "yy"], ["yy"])
                TS("dve", msk[:], yy[:], -math.pi, ALU.is_lt, ["yy"], ["msk"])
                STT(yy[:], msk[:], 2.0 * math.pi, yy[:], ALU.mult, ALU.add, ["msk", "yy"], ["yy"])
                TS("dve", yy[:], yy[:], math.pi, ALU.min, ["yy"], ["yy"], s2=-math.pi, op1=ALU.max)
                ACT(cs[:, :, which * 16:(which + 1) * 16], yy[:], AF.Sin, ["yy"], ["cs"])
            S.barrier()

        for b in range(NSEQ):
            with ExitStack() as ph:
                w_in = sbt(ph, "w_in_s", [128, 8, 2208], BF16)
                w_uq = sbt(ph, "w_uq_s", [128, 3, 768], BF16)
                w_ukv = sbt(ph, "w_ukv_s", [128, 2, 1024], BF16)
                for ck in range(8):
                    DMA("w_in", w_in[:, ck, 0:1104], w_in_d[ck * 128:(ck + 1) * 128, 0:1104], [], ["w_in"], queue="pool")
                    DMA("w_in", w_in[:, ck, 1104:2208], w_in_d[ck * 128:(ck + 1) * 128, 1104:2208], [], ["w_in"], queue="pool")
                for ck in range(3):
                    DMA("w_uq", w_uq[:, ck, :], w_uq_d[ck * 128:(ck + 1) * 128, :], [], ["w_uq"], queue="pool")
                for ck in range(2):
                    DMA("w_ukv", w_ukv[:, ck, :], w_ukv_d[ck * 128:(ck + 1) * 128, :], [], ["w_ukv"], queue="pool")
                if b == 0 and do_peer:
                    items = []
                    for i in range(128):
                        items.append((uT_d[:, i * 128:(i + 1) * 128].rearrange("(ck p) e -> p ck e", p=128), u_bf_d[i], True))
                        items.append((v_d[i * 128:(i + 1) * 128, :], v_bf_d[i], False))
                    for n in range(len(items) + 2):
                        if n < len(items):
                            src_ap, _, is_u = items[n]
                            sl = n % 4
                            dst_ap = stg[sl][:].rearrange("p (ck e) -> p ck e", ck=8) if is_u else stg[sl][:]
                            DMA(f"stgc{sl}", dst_ap, src_ap, [], [f"stg{sl}"], queue="pool")
                        m = n - 2
                        if m >= 0:
                            sl = m % 4
                            DMA(f"stgs{sl}", items[m][1], stg[sl][:], [f"stg{sl}"], [], queue="pool")
                xts = [sbt(ph, f"xt{i}", [128, 1024], F32) for i in range(2)]
                junk = sbt(ph, "junk", [128, 1024], F32)
                hns = [sbt(ph, f"hn{i}", [128, 1024], BF16) for i in range(2)]
                hTs = [sbt(ph, f"hT{i}", [128, 8, 128], BF16) for i in range(2)]
                sttfs = [sbt(ph, f"sttf{i}", [128, 4], F32) for i in range(2)]
                stt = sbt(ph, "stt", [128, 16], F32)
                junkb = sbt(ph, "junkb", [128, 384], F32)
                st8 = sbt(ph, "st8", [128, 32], F32)
                qk_e = [sbt(ph, f"qk_e{i}", [128, 4, 128], BF16) for i in range(2)]
                lats = [sbt(ph, f"lat{i}", [128, 672], F32) for i in range(2)]
                latn = sbt(ph, "latn", [128, 640], BF16)
                latT = sbt(ph, "latT", [128, 5, 128], BF16)
                q_sb = sbt(ph, "q_sb", [128, 8, 96], F32)
                kv_sb = sbt(ph, "kv_sb", [128, 8, 128], F32)
                tmpq = sbt(ph, "tmpq", [128, 8, 96], F32)
                kn = sbt(ph, "kn", [128, 8, 96], F32)
                rt = [sbt(ph, f"rt{i}", [128, 8, 16], F32) for i in range(4)]
                qr = sbt(ph, "qr", [128, 8, 96], BF16)
                kr = sbt(ph, "kr", [128, 8, 96], BF16)
                qT_t = sbt(ph, "qT_t", [96, 8, 128], BF16)
                kT_t = sbt(ph, "kT_t", [96, 8, 128], BF16)

                def rope(src, dst, j, rd, wr):
                    x1 = src[:, :, 64:80]
                    x2 = src[:, :, 80:96]
                    cosb = cs[:, j, 0:16].unsqueeze(1).broadcast_to([128, 8, 16])
                    sinb = cs[:, j, 16:32].unsqueeze(1).broadcast_to([128, 8, 16])
                    TT("dve", rt[0][:], x1, cosb, ALU.mult, [rd, "cs"], ["rt0"])
                    TT("dve", rt[1][:], x2, sinb, ALU.mult, [rd, "cs"], ["rt1"])
                    TT("dve", rt[2][:], x1, sinb, ALU.mult, [rd, "cs"], ["rt2"])
                    TT("dve", rt[3][:], x2, cosb, ALU.mult, [rd, "cs"], ["rt3"])
                    TT("dve", dst[:, :, 64:80], rt[0][:], rt[1][:], ALU.subtract, ["rt0", "rt1"], [wr])
                    TT("dve", dst[:, :, 80:96], rt[2][:], rt[3][:], ALU.add, ["rt2", "rt3"], [wr])

                def blk(tb):
                    j = b * NB + tb
                    u = tb % 2
                    hn = hns[u]
                    hT = hTs[u]
                    lat = lats[u]
                    sttf = sttfs[u]
                    hn_n, hT_n, lat_n, sttf_n = f"hn{u}", f"hT{u}", f"lat{u}", f"sttf{u}"
                    t0 = tb * 128
                    xt = xts[tb % 2]
                    xn = f"xt{tb % 2}"
                    ACT(junk[:], xt[:], AF.Square, [xn], ["junk", sttf_n], accum_out=sttf[:, 0:1])
                    ACT(sttf[:, 1:2], sttf[:, 0:1], AF.Sqrt, [sttf_n, "epsc"], [sttf_n], scale=1.0 / 1024, bias=epsc[:])
                    RECIP(sttf[:, 2:3], sttf[:, 1:2], [sttf_n], [sttf_n])
                    STT(hn[:], xt[:], sttf[:, 2:3], g_attn[:], ALU.mult, ALU.mult, [xn, sttf_n, "g_attn"], [hn_n])
                    if tb + 2 < NB:
                        DMA(xn, xt[:], x_d[b, t0 + 256:t0 + 384, :], [], [xn])
                    pT = PSB[0].rearrange("p (a t) -> p a t", a=8)
                    for ck in range(8):
                        TR(pT[:, ck, :], hn[:, ck * 128:(ck + 1) * 128], ident[:], [hn_n, "ident"], [PN[0]])
                    CP("act", hT[:], pT, [PN[0]], [hT_n])
                    for qi, (bank, col0, dst) in enumerate(((1, 0, qt_sb_d), (2, 512, kt_sb_d))):
                        pq = PS[bank][:].rearrange("p (a t) -> p a t", a=4)
                        for p4 in range(4):
                            for ck in range(8):
                                MM(pq[:, p4, :], w_in[:, ck, col0 + p4 * 128:col0 + (p4 + 1) * 128], hT[:, ck, :],
                                   ck == 0, ck == 7, ["w_in", hT_n], [PN[bank]])
                        CP("dve" if qi == 0 else "act", qk_e[qi][:], pq, [PN[bank]], [f"qk_e{qi}"])
                        DMA(f"qk_e{qi}", dst[b].rearrange("(p e) d t -> (e d) p t", e=2)[:, :, t0:t0 + 128],
                            qk_e[qi][:], [f"qk_e{qi}"], [f"qkscr{qi}"])
                    for ck in range(8):
                        MM(PS[3][:, 0:512], hT[:, ck, :], w_in[:, ck, 1024:1536], ck == 0, ck == 7,
                           ["w_in", hT_n], [PN[3]])
                    CP("act", vsb_all[:, tb, :], PS[3][:, 0:512], [PN[3]], ["vsb_all"])
                    for ck in range(8):
                        MM(PS[4][:, 0:512], hT[:, ck, :], w_in[:, ck, 1536:2048], ck == 0, ck == 7,
                           ["w_in", hT_n], [PN[4]])
                    for ck in range(8):
                        MM(PS[5][:, 0:160], hT[:, ck, :], w_in[:, ck, 2048:2208], ck == 0, ck == 7,
                           ["w_in", hT_n], [PN[5]])
                    CP("dve", lat[:, 0:512], PS[4][:, 0:512], [PN[4]], [lat_n])
                    CP("act", lat[:, 512:672], PS[5][:, 0:160], [PN[5]], [lat_n])
                    yield
                    pT6 = PSB[6].rearrange("p (a t) -> p a t", a=8)
                    ACT(junkb[:, 0:384], lat[:, 0:384], AF.Square, [lat_n], ["junkb", "stt"], accum_out=stt[:, 3:4])
                    ACT(junkb[:, 0:256], lat[:, 384:640], AF.Square, [lat_n], ["junkb", "stt"], accum_out=stt[:, 4:5])
                    ACT(junkb[:, 0:32], lat[:, 640:672], AF.Square, [lat_n], ["junkb", "stt"], accum_out=stt[:, 11:12])
                    TS("dve", stt[:, 5:6], stt[:, 3:4], 1.0 / 384, ALU.mult, ["stt"], ["stt"], s2=EPS, op1=ALU.add)
                    TS("dve", stt[:, 6:7], stt[:, 4:5], 1.0 / 256, ALU.mult, ["stt"], ["stt"], s2=EPS, op1=ALU.add)
                    ACT(stt[:, 7:9], stt[:, 5:7], AF.Sqrt, ["stt"], ["stt"])
                    RECIP(stt[:, 9:11], stt[:, 7:9], ["stt"], ["stt"])
                    STT(latn[:, 0:384], lat[:, 0:384], stt[:, 9:10], g_cq[:], ALU.mult, ALU.mult,
                        [lat_n, "stt", "g_cq"], ["latn"])
                    STT(latn[:, 384:640], lat[:, 384:640], stt[:, 10:11], g_ckv[:], ALU.mult, ALU.mult,
                        [lat_n, "stt", "g_ckv"], ["latn"])
                    for ck in range(5):
                        TR(pT6[:, ck, :], latn[:, ck * 128:(ck + 1) * 128], ident[:], ["latn", "ident"], [PN[6]])
                    CP("act", latT[:], pT6[:, 0:5, :], [PN[6]], ["latT"])
                    for ck in range(3):
                        MM(PS[6][:, 0:512], latT[:, ck, :], w_uq[:, ck, 0:512], ck == 0, ck == 2, ["latT", "w_uq"], [PN[6]])
                    for ck in range(3):
                        MM(PS[7][:, 0:256], latT[:, ck, :], w_uq[:, ck, 512:768], ck == 0, ck == 2, ["latT", "w_uq"], [PN[7]])
                    qf = q_sb[:].rearrange("p h d -> p (h d)")
                    CP("act", qf[:, 0:512], PS[6][:, 0:512], [PN[6]], ["q_sb"])
                    CP("dve", qf[:, 512:768], PS[7][:, 0:256], [PN[7]], ["q_sb"])
                    for ck in range(2):
                        MM(PS[6][:, 0:512], latT[:, 3 + ck, :], w_ukv[:, ck, 0:512], ck == 0, ck == 1, ["latT", "w_ukv"], [PN[6]])
                    for ck in range(2):
                        MM(PS[7][:, 0:512], latT[:, 3 + ck, :], w_ukv[:, ck, 512:1024], ck == 0, ck == 1, ["latT", "w_ukv"], [PN[7]])
                    kvf = kv_sb[:].rearrange("p h d -> p (h d)")
                    CP("act", kvf[:, 0:512], PS[6][:, 0:512], [PN[6]], ["kv_sb"])
                    CP("dve", kvf[:, 512:1024], PS[7][:, 0:512], [PN[7]], ["kv_sb"])
                    TT("dve", tmpq[:], q_sb[:], q_sb[:], ALU.mult, ["q_sb"], ["tmpq"])
                    RED(st8[:, 0:8], tmpq[:], ["tmpq"], ["st8"])
                    TS("dve", st8[:, 0:8], st8[:, 0:8], 1.0 / 96, ALU.mult, ["st8"], ["st8"], s2=EPS, op1=ALU.add)
                    ACT(st8[:, 8:16], st8[:, 0:8], AF.Sqrt, ["st8"], ["st8"])
                    RECIP(st8[:, 0:8], st8[:, 8:16], ["st8"], ["st8"])
                    TT("dve", tmpq[:], q_sb[:], st8[:, 0:8].unsqueeze(2).broadcast_to([128, 8, 96]), ALU.mult,
                       ["q_sb", "st8"], ["tmpq"])
                    TT("dve", tmpq[:], tmpq[:], g_q[:].unsqueeze(1).broadcast_to([128, 8, 96]), ALU.mult,
                       ["tmpq", "g_q"], ["tmpq"])
                    CP("dve", qr[:, :, 0:64], tmpq[:, :, 0:64], ["tmpq"], ["qr"])
                    rope(tmpq, qr, j, "tmpq", "qr")
                    pT2 = PSB[6].rearrange("p (a t) -> p a t", a=8)
                    for h in range(8):
                        TR(pT2[0:96, h, :], qr[:, h, :], ident[:], ["qr", "ident"], [PN[6]])
                    CP("act", qT_t[:], pT2[0:96, :, :], [PN[6]], ["qT_t"])
                    DMA("qT_t", qt_ml_d[b].rearrange("h d t -> d h t")[:, :, t0:t0 + 128], qT_t[:], ["qT_t"], ["qtml"])
                    TT("dve", kn[:, :, 0:64], kv_sb[:, :, 0:64], kv_sb[:, :, 0:64], ALU.mult, ["kv_sb"], ["kn"])
                    RED(st8[:, 16:24], kn[:, :, 0:64], ["kn"], ["st8"])
                    TS("dve", st8[:, 16:24], st8[:, 16:24], stt[:, 11:12], ALU.add, ["st8", "stt"], ["st8"],
                       s2=1.0 / 96, op1=ALU.mult)
                    TS("dve", st8[:, 16:24], st8[:, 16:24], EPS, ALU.add, ["st8"], ["st8"])
                    ACT(st8[:, 24:32], st8[:, 16:24], AF.Sqrt, ["st8"], ["st8"])
                    RECIP(st8[:, 16:24], st8[:, 24:32], ["st8"], ["st8"])
                    rk = st8[:, 16:24]
                    TT("dve", kn[:, :, 0:64], kv_sb[:, :, 0:64], rk.unsqueeze(2).broadcast_to([128, 8, 64]), ALU.mult,
                       ["kv_sb", "st8"], ["kn"])
                    TT("dve", kn[:, :, 64:96], lat[:, 640:672].unsqueeze(1).broadcast_to([128, 8, 32]),
                       rk.unsqueeze(2).broadcast_to([128, 8, 32]), ALU.mult, [lat_n, "st8"], ["kn"])
                    TT("dve", kn[:], kn[:], g_k[:].unsqueeze(1).broadcast_to([128, 8, 96]), ALU.mult,
                       ["kn", "g_k"], ["kn"])
                    CP("dve", kr[:, :, 0:64], kn[:, :, 0:64], ["kn"], ["kr"])
                    rope(kn, kr, j, "kn", "kr")
                    CP("act", vaug_all[:, tb, :, 0:64], kv_sb[:, :, 64:128], ["kv_sb"], ["vaug_all"])
                    pT3 = PSB[7].rearrange("p (a t) -> p a t", a=8)
                    for h in range(8):
                        TR(pT3[0:96, h, :], kr[:, h, :], ident[:], ["kr", "ident"], [PN[7]])
                    CP("dve", kT_t[:], pT3[0:96, :, :], [PN[7]], ["kT_t"])
                    DMA("kT_t", kt_ml_d[b].rearrange("h d t -> d h t")[:, :, t0:t0 + 128], kT_t[:], ["kT_t"], ["ktml"])

                DMA("xt0", xts[0][:], x_d[b, 0:128, :], [], ["xt0"])
                DMA("xt1", xts[1][:], x_d[b, 128:256, :], [], ["xt1"])
                gens = [blk(tb) for tb in range(NB)]

                def exhaust(g_):
                    for _ in g_:
                        pass

                S.emit_interleaved(S.record(lambda: next(gens[0])), [])
                for tb in range(NB):
                    fl = S.record(lambda: next(gens[tb + 1])) if tb + 1 < NB else []
                    bl = S.record(lambda: exhaust(gens[tb]))
                    S.emit_interleaved(fl, bl)
                S.barrier()

            with ExitStack() as ph:
                qh = [sbt(ph, f"qh{i}", [64, S_LEN], BF16) for i in range(2)]
                kh = [sbt(ph, f"kh{i}", [64, S_LEN], BF16) for i in range(2)]
                NS = 8
                fch = [sbt(ph, f"fch{i}", [128, 512], F32) for i in range(NS)]
                Pch = [sbt(ph, f"Pch{i}", [128, 513], F32) for i in range(NS)]
                Ach = [sbt(ph, f"Ach{i}", [128, 512], BF16) for i in range(NS)]
                ATc = [sbt(ph, f"ATc{i}", [128, 4, 128], BF16) for i in range(NS)]
                osb = [sbt(ph, f"osb{i}", [128, 64], F32) for i in range(2)]
                DMA("qh0", qh[0][:], qt_sb_d[b, 0], ["qkscr0"], ["qh0"])
                DMA("kh0", kh[0][:], kt_sb_d[b, 0], ["qkscr1"], ["kh0"])
                chunks = []
                itn = 0
                for h in range(8):
                    for qb0 in range(0, NB, 2):
                        rows = []
                        for qb in (qb0, qb0 + 1):
                            L = (qb + 1) * 128
                            nch = (L + 511) // 512
                            rows.append([(h, qb, c, min(512, L - c * 512), c == nch - 1, c == 0, itn + (qb - qb0))
                                         for c in range(nch - 1, -1, -1)])
                        itn += 2
                        last_idx = [None, None]
                        for k_ in range(max(len(rows[0]), len(rows[1]))):
                            for r_ in range(2):
                                if k_ < len(rows[r_]):
                                    chunks.append(rows[r_][k_] + (last_idx[r_],))
                                    last_idx[r_] = len(chunks) - 1
                seen_heads = set()

                def stA(n):
                    h, qb, c, w, first, last, it, prev = chunks[n]
                    hs = h % 2
                    if h not in seen_heads and h + 1 < 8:
                        seen_heads.add(h)
                        ns = (h + 1) % 2
                        DMA(f"qh{ns}", qh[ns][:], qt_sb_d[b, h + 1], ["qkscr0"], [f"qh{ns}"])
                        DMA(f"kh{ns}", kh[ns][:], kt_sb_d[b, h + 1], ["qkscr1"], [f"kh{ns}"])
                    bank = n % 3
                    sl = n % NS
                    MM(PS[bank][:, 0:w], qh[hs][:, qb * 128:(qb + 1) * 128], kh[hs][:, c * 512:c * 512 + w],
                       True, not first, [f"qh{hs}", f"kh{hs}"], [PN[bank]])
                    if first:
                        MM(PS[bank][:, w - 128:w], ident[:], negm[:], False, True, ["ident", "negm"], [PN[bank]])
                    ACT(fch[sl][:, 0:w], PS[bank][:, 0:w], AF.Sigmoid, [PN[bank]], [f"fch{sl}"], scale=-0.125)

                def stB(n):
                    h, qb, c, w, first, last, it, prev = chunks[n]
                    sl = n % NS
                    P = Pch[sl]
                    if first:
                        S.op("dve", lambda e, P=P, w=w, sl=sl: e.tensor_tensor_scan(
                            out=P[:, w - 1::-1], data0=fch[sl][:, w - 1::-1], data1=zero1[:].broadcast_to([128, w]),
                            initial=1.0, op0=ALU.mult, op1=ALU.bypass),
                            [f"fch{sl}", "zero1"], [f"Pch{sl}"])
                        MSET("pool", P[:, w:w + 1], 1.0, [f"Pcar{sl}"])
                    else:
                        pl = prev % NS
                        S.op("dve", lambda e, P=P, w=w, sl=sl, pl=pl: e.tensor_tensor_scan(
                            out=P[:, w - 1::-1], data0=fch[sl][:, w - 1::-1], data1=zero1[:].broadcast_to([128, w]),
                            initial=Pch[pl][:, 0:1], op0=ALU.mult, op1=ALU.bypass),
                            [f"fch{sl}", "zero1", f"Pch{pl}"], [f"Pch{sl}"])
                        CP("pool", P[:, w:w + 1], Pch[pl][:, 0:1], [f"Pch{pl}"], [f"Pcar{sl}"])
                    if w == 512:
                        TT("pool", Ach[sl][:, 0:352], P[:, 1:353], P[:, 0:352], ALU.subtract, [f"Pch{sl}"],
                           [f"Ach{sl}"])
                        TT("dve", Ach[sl][:, 352:512], P[:, 353:513], P[:, 352:512], ALU.subtract,
                           [f"Pch{sl}", f"Pcar{sl}"], [f"AchB{sl}"])
                    else:
                        TT("pool", Ach[sl][:, 0:w], P[:, 1:w + 1], P[:, 0:w], ALU.subtract, [f"Pch{sl}", f"Pcar{sl}"],
                           [f"Ach{sl}", f"AchB{sl}"])

                def stC(n):
                    h, qb, c, w, first, last, it, prev = chunks[n]
                    sl = n % NS
                    bank = 3 + n % 2
                    nb_ = w // 128
                    pv = PSB[bank].rearrange("p (a t) -> p a t", a=8)
                    for k_ in range(nb_):
                        TR(pv[:, k_, :], Ach[sl][:, k_ * 128:(k_ + 1) * 128], ident[:], [f"Ach{sl}", f"AchB{sl}", "ident"], [PN[bank]])
                    CP("act", ATc[sl][:, 0:nb_, :], pv[:, 0:nb_, :], [PN[bank]], [f"ATc{sl}"])

                def stD(n):
                    h, qb, c, w, first, last, it, prev = chunks[n]
                    sl = n % NS
                    ob = 5 + it % 2
                    nb_ = w // 128
                    for k_ in range(nb_):
                        kbk = c * 4 + k_
                        MM(PS[ob][:, 0:64], ATc[sl][:, k_, :], vsb_all[:, kbk, h * 64:(h + 1) * 64],
                           first and k_ == 0, last and k_ == nb_ - 1, [f"ATc{sl}", "vsb_all"], [PN[ob]])
                    if last:
                        ps_ = it % 2
                        CP("dve", osb[ps_][:], PS[ob][:, 0:64], [PN[ob]], [f"osb{ps_}"])
                        DMA(f"osb{ps_}", mixraw_d[b, qb * 128:(qb + 1) * 128, h * 64:(h + 1) * 64], osb[ps_][:],
                            [f"osb{ps_}"], [])

                NCH = len(chunks)
                for t in range(NCH + 4):
                    if t < NCH:
                        stA(t)
                    if 0 <= t - 1 < NCH:
                        stB(t - 1)
                    if 0 <= t - 3 < NCH:
                        stC(t - 3)
                    if 0 <= t - 4 < NCH:
                        stD(t - 4)
                S.barrier()

            with ExitStack() as ph:
                qh = [sbt(ph, f"mqh{i}", [96, S_LEN], BF16) for i in range(2)]
                kh = [sbt(ph, f"mkh{i}", [96, S_LEN], BF16) for i in range(2)]
                PT = [sbt(ph, f"PT{i}", [128, 512], BF16) for i in range(3)]
                om = [sbt(ph, f"om{i}", [128, 4, 64], F32) for i in range(2)]
                rec = [sbt(ph, f"rec{i}", [128, 4], F32) for i in range(2)]
                sc_mla = 96.0 ** -0.5
                it = 0
                gi = 0
                DMA("mqh0", qh[0][:], qt_ml_d[b, 0], ["qtml"], ["mqh0"])
                DMA("mkh0", kh[0][:], kt_ml_d[b, 0], ["ktml"], ["mkh0"])
                steps = [(h, qg, kb) for h in range(8) for qg in range(8) for kb in range(4 * qg + 4)]

                def s_step(n):
                    h, qg, kb = steps[n]
                    hs = h % 2
                    if qg == 0 and kb == 0 and h + 1 < 8:
                        ns = (h + 1) % 2
                        DMA(f"mqh{ns}", qh[ns][:], qt_ml_d[b, h + 1], ["qtml"], [f"mqh{ns}"])
                        DMA(f"mkh{ns}", kh[ns][:], kt_ml_d[b, h + 1], ["ktml"], [f"mkh{ns}"])
                    jj = kb - 4 * qg
                    c0 = 128 * jj if jj > 0 else 0
                    sb_ = n % 3
                    MM(PS[sb_][:, c0:512], kh[hs][:, kb * 128:(kb + 1) * 128],
                       qh[hs][:, qg * 512 + c0:qg * 512 + 512], True, True,
                       [f"mqh{hs}", f"mkh{hs}"], [PN[sb_]])
                    ACT(PT[sb_][:, c0:512], PS[sb_][:, c0:512], AF.Exp, [PN[sb_]], [f"PT{sb_}"], scale=sc_mla)
                    if jj >= 0:
                        MSET("pool", PT[sb_][64:128, c0:c0 + 64], 0.0, [f"PT{sb_}"])

                def pv_step(n):
                    h, qg, kb = steps[n]
                    gi = h * 8 + qg
                    ab = 3 + gi % 2
                    accn = PN[ab]
                    acc = PS[ab][:, 0:260].rearrange("p (q d) -> p q d", q=4)
                    if kb == 0:
                        MM(PS[ab][:, 0:260], zeros_bf[:, 0:128], zeros_bf[:, 0:260], True, False, ["zeros_bf"], [accn])
                    jj = kb - 4 * qg
                    sb_ = n % 3
                    for qq in range(max(jj, 0), 4):
                        MM(acc[:, qq, :], PT[sb_][:, qq * 128:(qq + 1) * 128], vaug_all[:, kb, h, :], False,
                           kb == 4 * qg + qq, [f"PT{sb_}", "vaug_all"], [accn])
                    if kb == 4 * qg + 3:
                        os_ = gi % 2
                        RECIP(rec[os_][:].unsqueeze(2), acc[:, :, 64:65], [accn], [f"rec{os_}"])
                        TT("dve", om[os_][:], acc[:, :, 0:64], rec[os_][:].unsqueeze(2).broadcast_to([128, 4, 64]),
                           ALU.mult, [accn, f"rec{os_}"], [f"om{os_}"])
                        DMA(f"om{os_}",
                            mixraw_d[b, qg * 512:(qg + 1) * 512, 512 + h * 64:512 + (h + 1) * 64].rearrange(
                                "(q p) d -> p q d", p=128),
                            om[os_][:], [f"om{os_}"], ["mixraw"])

                s_step(0)
                for n in range(len(steps)):
                    if n + 1 < len(steps):
                        s_step(n + 1)
                    pv_step(n)
                S.barrier()

            with ExitStack() as ph:
                w_o = sbt(ph, "w_o_s", [128, 8, 1024], BF16)
                for ck in range(8):
                    DMA("w_o", w_o[:, ck, :], w_o_d[ck * 128:(ck + 1) * 128, :], [], ["w_o"], queue="pool")
                xts = [sbt(ph, f"xo{i}", [128, 1024], F32) for i in range(2)]
                mrs = [sbt(ph, f"mr{i}", [128, 1024], F32) for i in range(2)]
                junks = [sbt(ph, f"junk4{i}", [128, 512], F32) for i in range(2)]
                stts = [sbt(ph, f"stt4{i}", [128, 8], F32) for i in range(2)]
                mixns = [sbt(ph, f"mixn{i}", [128, 1024], BF16) for i in range(2)]
                mixTs = [sbt(ph, f"mixT{i}", [128, 8, 128], BF16) for i in range(2)]
                x2s = [sbt(ph, f"x2s{i}", [128, 1024], F32) for i in range(2)]
                for tb in range(NB):
                    t0 = tb * 128
                    s_ = tb % 2
                    junk, stt, mixn, mixT = junks[s_], stts[s_], mixns[s_], mixTs[s_]
                    jn, sn, mn, mTn = f"junk4{s_}", f"stt4{s_}", f"mixn{s_}", f"mixT{s_}"
                    tbank = 0 if s_ == 0 else 5
                    if tb == 0:
                        DMA("xo0", xts[0][:], x_d[b, 0:128, :], [], ["xo0"])
                        DMA("mr0", mrs[0][:], mixraw_d[b, 0:128, :], ["mixraw"], ["mr0"])
                    if tb + 1 < NB:
                        n_ = (tb + 1) % 2
                        t1 = (tb + 1) * 128
                        DMA(f"xo{n_}", xts[n_][:], x_d[b, t1:t1 + 128, :], [], [f"xo{n_}"])
                        DMA(f"mr{n_}", mrs[n_][:], mixraw_d[b, t1:t1 + 128, :], ["mixraw"], [f"mr{n_}"])
                    mr = mrs[s_]
                    ACT(junk[:], mr[:, 0:512], AF.Square, [f"mr{s_}"], [jn, sn], accum_out=stt[:, 0:1])
                    ACT(junk[:], mr[:, 512:1024], AF.Square, [f"mr{s_}"], [jn, sn], accum_out=stt[:, 1:2])
                    ACT(stt[:, 2:4], stt[:, 0:2], AF.Sqrt, [sn, "epsc"], [sn], scale=1.0 / 512, bias=epsc[:])
                    RECIP(stt[:, 4:6], stt[:, 2:4], [sn], [sn])
                    STT(mixn[:, 0:512], mr[:, 0:512], stt[:, 4:5], g_sbo[:], ALU.mult, ALU.mult,
                        [f"mr{s_}", sn, "g_sbo"], [mn])
                    STT(mixn[:, 512:1024], mr[:, 512:1024], stt[:, 5:6], g_mlao[:], ALU.mult, ALU.mult,
                        [f"mr{s_}", sn, "g_mlao"], [mn])
                    pT = PSB[tbank].rearrange("p (a t) -> p a t", a=8)
                    for ck in range(8):
                        TR(pT[:, ck, :], mixn[:, ck * 128:(ck + 1) * 128], ident[:], [mn, "ident"], [PN[tbank]])
                    CP("act", mixT[:], pT, [PN[tbank]], [mTn])
                    for half in range(2):
                        bank = 1 + 2 * s_ + half
                        for ck in range(8):
                            MM(PS[bank][:, 0:512], mixT[:, ck, :], w_o[:, ck, half * 512:(half + 1) * 512], ck == 0,
                               ck == 7, [mTn, "w_o"], [PN[bank]])
                        TT("dve", x2s[s_][:, half * 512:(half + 1) * 512], PS[bank][:, 0:512],
                           xts[s_][:, half * 512:(half + 1) * 512], ALU.add, [PN[bank], f"xo{s_}"], [f"x2s{s_}"])
                    DMA(f"x2s{s_}", out_d[b, t0:t0 + 128, :], x2s[s_][:], [f"x2s{s_}"], ["out"])
                S.barrier()

        ast.close()
        if do_peer:
            TG = 256
            w_pq = sbt(st, "w_pq_s", [128, 8, 2048], BF16)
            skT = sbt(st, "skT_s", [128, 16, 128], BF16)
            for ck in range(8):
                DMA("w_pq", w_pq[:, ck, :], w_pq_d[ck * 128:(ck + 1) * 128, :], [], ["w_pq"], queue="pool")
            DMA("skT", skT[:], skT_d, [], ["skT"], queue="pool")
            iota_b = sbt(st, "iota_b", [128, 128], BF16)
            CP("dve", iota_b[:], iota_f[:], ["iota_f"], ["iota_b"])
            GT = sbt(st, "GT", [128, 128, TG], BF16)
            hn2Ts = [sbt(st, f"hn2T{i}", [128, 8, TG], BF16) for i in range(2)]
            ijgTs = [sbt(st, f"ijgT{i}", [128, 3, TG], F32) for i in range(2)]
            x2r = sbt(st, "x2r", [128, 1024], F32)
            stt = sbt(st, "pstt", [128, 8], F32)
            hn2 = sbt(st, "hn2", [128, 1024], BF16)
            qT = sbt(st, "pqT", [128, 16, 128], BF16)
            sc = sbt(st, "psc", [128, 16, 128], F32)
            sc2 = sbt(st, "psc2", [128, 128], F32)
            tops = sbt(st, "tops", [128, 16, 16], F32)
            topi = sbt(st, "topi", [128, 16, 16], U32)
            topif = sbt(st, "topif", [128, 16, 16], F32)
            cand = sbt(st, "cand", [128, 8, 256], F32)
            cand2 = sbt(st, "cand2", [128, 256], F32)
            best = sbt(st, "best", [128, 8, 16], F32)
            posu = sbt(st, "posu", [128, 8, 16], U32)
            posf = sbt(st, "pposf", [128, 8, 16], F32)
            af = sbt(st, "paf", [128, 8, 16], F32)
            bf = sbt(st, "pbf", [128, 8, 16], F32)
            oh = sbt(st, "poh", [128, 8, 16, 16], F32)
            ijg = sbt(st, "pijg", [128, 3, 8, 16], F32)
            sm = sbt(st, "psm", [128, 16], F32)
            wA = [sbt(st, f"wA{i}", [128, 128], BF16) for i in range(8)]
            wB = [sbt(st, f"wB{i}", [128, 128], BF16) for i in range(8)]
            NU = 4
            ub = [sbt(st, f"ub{i}", [128, 8, 128], BF16) for i in range(NU)]
            vb = [sbt(st, f"vb{i}", [128, 1024], BF16) for i in range(NU)]
            ge = [sbt(st, f"ge{i}", [128, TG], BF16) for i in range(2)]
            cf = [sbt(st, f"cf{i}", [128, TG], BF16) for i in range(2)]
            fin = [sbt(st, f"fin{i}", [128, 1024], F32) for i in range(2)]

            groups = [(b, g) for b in range(NSEQ) for g in range(S_LEN // TG)]

            def routing(gi):
                b, g = groups[gi]
                par = gi % 2
                hn2T = hn2Ts[par]
                hTn = f"hn2T{par}"
                ijgT = ijgTs[par]
                iTn = f"ijgT{par}"
                tg0 = g * TG
                junk = oh[:].rearrange("p h k a -> p (h k a)")[:, 0:1024]
                for tk in range(2):
                    t0 = tg0 + tk * 128
                    DMA("x2r", x2r[:], out_d[b, t0:t0 + 128, :], [f"out{b}_{g}"], ["x2r"])
                    for _ in range(6):
                        yield
                    ACT(junk, x2r[:], AF.Square, ["x2r"], ["poh", "pstt"], accum_out=stt[:, 0:1])
                    ACT(stt[:, 1:2], stt[:, 0:1], AF.Sqrt, ["pstt", "epsc"], ["pstt"], scale=1.0 / 1024, bias=epsc[:])
                    RECIP(stt[:, 2:3], stt[:, 1:2], ["pstt"], ["pstt"])
                    STT(hn2[:], x2r[:], stt[:, 2:3], g_ffn[:], ALU.mult, ALU.mult, ["x2r", "pstt", "g_ffn"], ["hn2"])
                    for _ in range(8):
                        yield
                    pT = PSB[2].rearrange("p (a t) -> p a t", a=8)
                    for ck in range(8):
                        TR(pT[:, ck, :], hn2[:, ck * 128:(ck + 1) * 128], ident[:], ["hn2", "ident"], [PN[2]])
                    CP("act", hn2T[:, :, tk * 128:(tk + 1) * 128], pT, [PN[2]], [hTn])
                    yield
                    for r in range(4):
                        bank = 2 + r % 2
                        pq = PS[bank][:].rearrange("p (a n) -> p a n", a=4)
                        for a4 in range(4):
                            hc = r * 4 + a4
                            for ck in range(8):
                                MM(pq[:, a4, :], w_pq[:, ck, hc * 128:(hc + 1) * 128], hn2T[:, ck, tk * 128:(tk + 1) * 128],
                                   ck == 0, ck == 7, ["w_pq", hTn], [PN[bank]])
                            yield
                        CP("act", qT[:, r * 4:(r + 1) * 4, :], pq, [PN[bank]], ["pqT"])
                    for r in range(4):
                        bank = 2 + r % 2
                        pq = PS[bank][:].rearrange("p (a n) -> p a n", a=4)
                        for a4 in range(4):
                            hc = r * 4 + a4
                            MM(pq[:, a4, :], qT[:, hc, :], skT[:, hc, :], True, True, ["pqT", "skT"], [PN[bank]])
                        CP("act", sc[:, r * 4:(r + 1) * 4, :], pq, [PN[bank]], ["psc"])
                        yield
                    for hc in range(16):
                        S.op("dve", lambda e, hc=hc: e.max(out=tops[:, hc, 0:8], in_=sc[:, hc, :]), ["psc"], ["tops"])
                        S.op("dve", lambda e, hc=hc: e.max_index(out=topi[:, hc, 0:8], in_max=tops[:, hc, 0:8],
                                                                   in_values=sc[:, hc, :]), ["psc", "tops"], ["topi"])
                        S.op("dve", lambda e, hc=hc: e.match_replace(out=sc2[:], in_to_replace=tops[:, hc, 0:8],
                                                                       in_values=sc[:, hc, :], imm_value=-1e30),
                             ["psc", "tops"], ["psc2"])
                        yield
                        S.op("dve", lambda e, hc=hc: e.max(out=tops[:, hc, 8:16], in_=sc2[:]), ["psc2"], ["tops"])
                        S.op("dve", lambda e, hc=hc: e.max_index(out=topi[:, hc, 8:16], in_max=tops[:, hc, 8:16],
                                                                   in_values=sc2[:]), ["psc2", "tops"], ["topi"])
                        yield
                    CP("dve", topif[:], topi[:], ["topi"], ["topif"])
                    tops4 = tops[:].rearrange("p (h c) k -> p h c k", c=2)
                    topif4 = topif[:].rearrange("p (h c) k -> p h c k", c=2)
                    cand4 = cand[:].rearrange("p h (a b) -> p h a b", a=16)
                    TT("dve", cand4, tops4[:, :, 0, :].unsqueeze(3).broadcast_to([128, 8, 16, 16]),
                       tops4[:, :, 1, :].unsqueeze(2).broadcast_to([128, 8, 16, 16]), ALU.add, ["tops"], ["cand"])
                    yield
                    for h in range(8):
                        S.op("dve", lambda e, h=h: e.max(out=best[:, h, 0:8], in_=cand[:, h, :]), ["cand"], ["best"])
                        S.op("dve", lambda e, h=h: e.max_index(out=posu[:, h, 0:8], in_max=best[:, h, 0:8],
                                                                 in_values=cand[:, h, :]), ["cand", "best"], ["posu"])
                        S.op("dve", lambda e, h=h: e.match_replace(out=cand2[:], in_to_replace=best[:, h, 0:8],
                                                                     in_values=cand[:, h, :], imm_value=-1e30),
                             ["cand", "best"], ["cand2"])
                        yield
                        S.op("dve", lambda e, h=h: e.max(out=best[:, h, 8:16], in_=cand2[:]), ["cand2"], ["best"])
                        S.op("dve", lambda e, h=h: e.max_index(out=posu[:, h, 8:16], in_max=best[:, h, 8:16],
                                                                 in_values=cand2[:]), ["cand2", "best"], ["posu"])
                        yield
                    gsm = ijg[:, 2, :, :]
                    TT("dve", gsm, best[:], best[:, :, 0:1].broadcast_to([128, 8, 16]), ALU.subtract, ["best"], ["pijg"])
                    for _ in range(16):
                        yield
                    ACT(gsm, gsm, AF.Exp, ["pijg"], ["pijg"])
                    RED(sm[:, 0:8], gsm, ["pijg"], ["psm"])
                    RECIP(sm[:, 8:16], sm[:, 0:8], ["psm"], ["psm"])
                    TT("dve", gsm, gsm, sm[:, 8:16].unsqueeze(2).broadcast_to([128, 8, 16]), ALU.mult, ["pijg", "psm"], ["pijg"])
                    yield
                    CP("dve", posf[:], posu[:], ["posu"], ["pposf"])
                    oh15 = oh[:].rearrange("p h k a -> p (h k) a")[:, :, 0:15]
                    TT("dve", oh15, posf[:].rearrange("p h k -> p (h k)").unsqueeze(2).broadcast_to([128, 128, 15]),
                       thr_f[:].unsqueeze(1).broadcast_to([128, 128, 15]), ALU.is_ge, ["pposf", "thr_f"], ["poh"])
                    RED(af[:].rearrange("p h k -> p (h k)"), oh15, ["poh"], ["paf"])
                    STT(bf[:], af[:], -16.0, posf[:], ALU.mult, ALU.add, ["paf", "pposf"], ["pbf"])
                    yield
                    for w_, src in ((0, af), (1, bf)):
                        TT("dve", oh[:], src[:].unsqueeze(3).broadcast_to([128, 8, 16, 16]),
                           iota_f[:, 0:16].unsqueeze(1).unsqueeze(1).broadcast_to([128, 8, 16, 16]), ALU.is_equal,
                           ["paf", "pbf", "iota_f"], ["poh"])
                        TT("dve", oh[:], oh[:], topif4[:, :, w_, :].unsqueeze(2).broadcast_to([128, 8, 16, 16]), ALU.mult,
                           ["poh", "topif"], ["poh"])
                        RED(ijg[:, w_, :, :], oh[:], ["poh"], ["pijg"])
                        yield
                    for _ in range(18):
                        yield
                    for w_ in range(3):
                        TR(PS[3][:, w_ * 128:(w_ + 1) * 128], ijg[:, w_, :, :].rearrange("p h k -> p (h k)"), identf[:],
                           ["pijg", "identf"], [PN[3]])
                    CP("act", ijgT[:, :, tk * 128:(tk + 1) * 128], PS[3][:, 0:384].rearrange("p (w t) -> p w t", w=3),
                       [PN[3]], [iTn])
                    yield

            def drain(gen):
                if gen is not None:
                    for _ in gen:
                        pass

            def step(gen, n):
                if gen is None:
                    return None
                for _ in range(n):
                    try:
                        next(gen)
                    except StopIteration:
                        return None
                return gen

            drain(routing(0))
            for gi, (b, g) in enumerate(groups):
                par = gi % 2
                hn2T = hn2Ts[par]
                hTn = f"hn2T{par}"
                ijgT = ijgTs[par]
                iTn = f"ijgT{par}"
                tg0 = g * TG
                for t in range(TG):
                    s4 = t % 4
                    s8 = t % 8
                    if s4 == 0:
                        grp = [f"wA{s8 + q_}" for q_ in range(4)] + [f"wB{s8 + q_}" for q_ in range(4)]
                        wra, wrb = grp, [f"wB{s8}"]
                    else:
                        wra, wrb = [f"wA{s8}"], [f"wB{s8}"]
                    TS("dve", wA[s8][:], iota_b[:], ijgT[:, 0, t:t + 1], ALU.is_equal, ["iota_b", iTn], wra)
                    TS("dve", wB[s8][:], iota_b[:], ijgT[:, 1, t:t + 1], ALU.is_equal, ["iota_b", iTn], wrb,
                       s2=ijgT[:, 2, t:t + 1], op1=ALU.mult)
                    bank = 2 + (t // 4) % 2
                    MM(PS[bank][:, s4 * 128:(s4 + 1) * 128], wB[s8][:], wA[s8][:], True, True,
                       [f"wA{s8}", f"wB{s8}"], [PN[bank]])
                    if s4 == 3:
                        CP("act", GT[:, :, t - 3:t + 1], PS[bank][:].rearrange("p (t i) -> p i t", t=4), [PN[bank]], ["GT"])
                for tk in range(2):
                    t0 = tg0 + tk * 128
                    DMA(f"finl{tk}", fin[tk][:], out_d[b, t0:t0 + 128, :], [f"out{b}_{g}"], [f"fin{tk}"])
                rgen = routing(gi + 1) if gi + 1 < len(groups) else None

                def loads(i):
                    s_ = i % NU
                    DMA(f"ub{s_}", ub[s_][:].rearrange("p a e -> p (a e)"), u_bf_d[i], [], [f"ub{s_}"])
                    DMA(f"vb{s_}", vb[s_][:], v_bf_d[i], [], [f"vb{s_}"])

                def stage1(i):
                    s_ = i % NU
                    pb = i % 2
                    for ck in range(8):
                        MM(PS[pb][:, 0:TG], ub[s_][:, ck, :], hn2T[:, ck, :], ck == 0, ck == 7, [f"ub{s_}", hTn], [PN[pb]])
                    ACT(ge[pb][:], PS[pb][:, 0:TG], AF.Gelu, [PN[pb]], [f"ge{pb}"])
                    TT("pool", cf[pb][:], ge[pb][:], GT[:, i, :], ALU.mult, [f"ge{pb}", "GT"], [f"cf{pb}"])

                def stage2(i):
                    s_ = i % NU
                    pb = i % 2
                    for th in range(2):
                        for dh in range(2):
                            bank = 4 + th * 2 + dh
                            MM(PS[bank][:, 0:512], cf[pb][:, th * 128:(th + 1) * 128], vb[s_][:, dh * 512:(dh + 1) * 512],
                               i == 0, i == 127, [f"vb{s_}", f"cf{pb}"], [PN[bank]])

                for i in range(min(3, 128)):
                    loads(i)
                stage1(0)
                for i in range(128):
                    if i + 3 < 128:
                        loads(i + 3)
                    if i + 1 < 128:
                        stage1(i + 1)
                    stage2(i)
                    rgen = step(rgen, 2)
                drain(rgen)
                for tk in range(2):
                    t0 = tg0 + tk * 128
                    for half in range(2):
                        bank = 4 + tk * 2 + half
                        TT("dve", fin[tk][:, half * 512:(half + 1) * 512], PS[bank][:, 0:512],
                           fin[tk][:, half * 512:(half + 1) * 512], ALU.add, [PN[bank], f"fin{tk}"], [f"fin{tk}"])
                    DMA(f"fins{tk}", out_d[b, t0:t0 + 128, :], fin[tk][:], [f"fin{tk}"], [f"out{b}_{g}"])

        S.barrier()
        with nc.Block() as block:
            S.replay(block)
    return nc


_CACHE = {}


def kernel(x, positions, attn_norm, w_in, cq_norm, w_uq, ckv_norm, w_ukv, q_norm, k_norm,
           sb_out_norm, mla_out_norm, w_o, ffn_norm, peer_w_q, peer_sub_keys, peer_u, peer_v):
    n = 8
    f32 = lambda a: np.ascontiguousarray(np.asarray(a), dtype=np.float32)
    x = f32(x)
    positions = np.asarray(positions).astype(np.int32)
    half = 16
    invf = (1.0 / (10000.0 ** (np.arange(half, dtype=np.float32) * np.float32(2.0 / 32)))).astype(np.float32)
    shared = {
        "invf": invf.reshape(1, 16),
        "g_attn": f32(attn_norm[0]).reshape(1, -1),
        "w_in": f32(w_in[0]),
        "g_cq": f32(cq_norm[0]).reshape(1, -1),
        "w_uq": f32(w_uq[0]),
        "g_ckv": f32(ckv_norm[0]).reshape(1, -1),
        "w_ukv": f32(w_ukv[0]),
        "g_q": f32(q_norm[0]).reshape(1, -1),
        "g_k": f32(k_norm[0]).reshape(1, -1),
        "g_sbo": f32(sb_out_norm[0]).reshape(1, -1),
        "g_mlao": f32(mla_out_norm[0]).reshape(1, -1),
        "w_o": f32(w_o[0]),
        "g_ffn": f32(ffn_norm[0]).reshape(1, -1),
        "w_pq": f32(peer_w_q[0]),
        "skT": f32(np.transpose(np.asarray(peer_sub_keys[0]), (3, 1, 0, 2)).reshape(128, 16, 128)),
        "uT": f32(np.asarray(peer_u[0]).T),
        "pv": f32(peer_v[0]),
    }
    in_maps = []
    for c in range(n):
        m = dict(shared)
        m["x"] = np.ascontiguousarray(x[NSEQ * c:NSEQ * (c + 1)])
        p = positions[NSEQ * c:NSEQ * (c + 1)].reshape(NSEQ, NB, 128)
        m["pos"] = np.ascontiguousarray(np.transpose(p, (2, 0, 1)).reshape(128, NSEQ * NB))
        in_maps.append(m)
    if "nc" not in _CACHE:
        _CACHE["nc"] = build_program()
    res = run_bass_kernel_spmd(_CACHE["nc"], in_maps, core_ids=list(range(n)))
    out = np.concatenate([np.asarray(r["out"]) for r in res.results], axis=0)
    return out.astype(np.float32)
```

```python
import math
from contextlib import ExitStack

import numpy as np
import concourse.bass as bass
import concourse.mybir as mybir
from concourse.bass_utils import run_bass_kernel_spmd

F32 = mybir.dt.float32
BF16 = mybir.dt.bfloat16
I32 = mybir.dt.int32
U32 = mybir.dt.uint32
AF = mybir.ActivationFunctionType
ALU = mybir.AluOpType
AX = mybir.AxisListType

ENGS = ["pe", "act", "dve", "pool", "sp"]
NSEQ = 2
S_LEN = 4096
NB = 32
D = 1024
EPS = 1e-6
BIG = 3000.0


class Buf:
    __slots__ = ("name", "w", "readers")

    def __init__(self, name):
        self.name = name
        self.w = None
        self.readers = []


class Sched:
    def __init__(self, nc, stack):
        self.nc = nc
        self.stack = stack
        self.q = {e: [] for e in ENGS}
        self.cnt = {e: 0 for e in ENGS}
        self.sem = {e: stack.enter_context(nc.semaphore("s_" + e)) for e in ENGS}
        self.waited = {e: {} for e in ENGS}
        self.dsem = {}
        self.dcnt = {}
        self.bufs = {}
        self.rec = None

    def record(self, thunk):
        self.rec = []
        try:
            thunk()
        finally:
            r = self.rec
            self.rec = None
        return r

    def emit(self, item):
        kind, a, fn, reads, writes, queue = item
        if kind == 0:
            self.op(a, fn, reads, writes)
        else:
            self.dma(a, fn, reads, writes, queue=queue)

    def emit_interleaved(self, la, lb):
        ia = ib = 0
        na, nb = len(la), len(lb)
        while ia < na or ib < nb:
            if ib >= nb or (ia < na and ia * nb <= ib * na):
                self.emit(la[ia])
                ia += 1
            else:
                self.emit(lb[ib])
                ib += 1

    def B(self, name):
        b = self.bufs.get(name)
        if b is None:
            b = self.bufs[name] = Buf(name)
        return b

    def _bl(self, xs):
        if isinstance(xs, str):
            return [self.B(xs)]
        return [self.B(x) for x in xs if x is not None]

    def _deps(self, eng, reads, writes):
        deps = {}
        for b in reads:
            if b.w is not None:
                k, v = b.w
                if deps.get(k, 0) < v:
                    deps[k] = v
        for b in writes:
            if b.w is not None:
                k, v = b.w
                if k != eng and deps.get(k, 0) < v:
                    deps[k] = v
            for (k, v) in b.readers:
                if deps.get(k, 0) < v:
                    deps[k] = v
        out = []
        wd = self.waited[eng]
        for k, v in deps.items():
            if k == eng and eng == "pe":
                continue
            if wd.get(k, 0) >= v:
                continue
            wd[k] = v
            out.append((k, v))
        return out

    def _mark(self, tok, reads, writes):
        for b in writes:
            b.w = tok
            b.readers = []
        for b in reads:
            if b.w is tok:
                continue
            rs = b.readers
            for i, (k, v) in enumerate(rs):
                if k == tok[0]:
                    rs[i] = tok
                    break
            else:
                rs.append(tok)

    def op(self, eng, fn, reads=(), writes=()):
        if self.rec is not None:
            self.rec.append((0, eng, fn, reads, writes, None))
            return None
        reads = self._bl(reads)
        writes = self._bl(writes)
        waits = self._deps(eng, reads, writes)
        self.cnt[eng] += 1
        tok = (eng, self.cnt[eng])
        self.q[eng].append((waits, fn, self.sem[eng], 1))
        self._mark(tok, reads, writes)
        return tok

    def dma(self, key, fn, reads=(), writes=(), queue="sp"):
        if self.rec is not None:
            self.rec.append((1, key, fn, reads, writes, queue))
            return None
        reads = self._bl(reads)
        writes = self._bl(writes)
        k = "d_" + key
        if k not in self.dsem:
            self.dsem[k] = self.stack.enter_context(self.nc.semaphore(k))
            self.dcnt[k] = 0
        waits = self._deps(queue, reads, writes)
        self.dcnt[k] += 16
        tok = (k, self.dcnt[k])
        self.q[queue].append((waits, fn, self.dsem[k], 16))
        self._mark(tok, reads, writes)
        return tok

    def barrier(self):
        for eng in ENGS:
            waits = []
            wd = self.waited[eng]
            for k, v in self.dcnt.items():
                if v and wd.get(k, 0) < v:
                    waits.append((k, v))
                    wd[k] = v
            for e in ENGS:
                if e == eng:
                    continue
                v = self.cnt[e]
                if v and wd.get(e, 0) < v:
                    waits.append((e, v))
                    wd[e] = v
            if waits:
                self.q[eng].append((waits, None, None, 0))

    def replay(self, block):
        s = self

        def semof(k):
            return s.sem[k] if k in s.sem else s.dsem[k]

        def run(name, eng):
            for waits, fn, sem, inc in s.q[name]:
                for k, v in waits:
                    eng.wait_ge(semof(k), v)
                if fn is not None:
                    fn(eng).then_inc(sem, inc)

        @block.tensor
        def _(e):
            run("pe", e)

        @block.scalar
        def _(e):
            run("act", e)

        @block.vector
        def _(e):
            run("dve", e)

        @block.gpsimd
        def _(e):
            run("pool", e)

        @block.sync
        def _(e):
            run("sp", e)


def build_program(do_peer=True):
    nc = bass.Bass("TRN2", target_bir_lowering=False)
    dt_in = lambda n, s, d=F32: nc.dram_tensor(n, list(s), d, kind="ExternalInput").ap()
    x_d = dt_in("x", [NSEQ, S_LEN, D])
    pos_d = dt_in("pos", [128, NSEQ * NB], I32)
    invf_d = dt_in("invf", [1, 16])
    g_attn_d = dt_in("g_attn", [1, 1024])
    w_in_d = dt_in("w_in", [1024, 2208])
    g_cq_d = dt_in("g_cq", [1, 384])
    w_uq_d = dt_in("w_uq", [384, 768])
    g_ckv_d = dt_in("g_ckv", [1, 256])
    w_ukv_d = dt_in("w_ukv", [256, 1024])
    g_q_d = dt_in("g_q", [1, 96])
    g_k_d = dt_in("g_k", [1, 96])
    g_sbo_d = dt_in("g_sbo", [1, 512])
    g_mlao_d = dt_in("g_mlao", [1, 512])
    w_o_d = dt_in("w_o", [1024, 1024])
    g_ffn_d = dt_in("g_ffn", [1, 1024])
    w_pq_d = dt_in("w_pq", [1024, 2048])
    skT_d = dt_in("skT", [128, 16, 128])
    uT_d = dt_in("uT", [1024, 16384])
    v_d = dt_in("pv", [16384, 1024])
    out_d = nc.dram_tensor("out", [NSEQ, S_LEN, D], F32, kind="ExternalOutput").ap()

    scr = lambda n, s, d: nc.dram_tensor(n, list(s), d, kind="Internal").ap()
    qt_sb_d = scr("qt_sb", [NSEQ, 8, 64, S_LEN], BF16)
    kt_sb_d = scr("kt_sb", [NSEQ, 8, 64, S_LEN], BF16)
    qt_ml_d = scr("qt_ml", [NSEQ, 8, 96, S_LEN], BF16)
    kt_ml_d = scr("kt_ml", [NSEQ, 8, 96, S_LEN], BF16)
    mixraw_d = scr("mixraw", [NSEQ, S_LEN, D], F32)
    u_bf_d = scr("u_bf", [128, 128, 1024], BF16)
    v_bf_d = scr("v_bf", [128, 128, 1024], BF16)

    with ExitStack() as st:
        S = Sched(nc, st)

        uid = [0]

        def sbt(stack, n, s, d):
            uid[0] += 1
            return stack.enter_context(nc.sbuf_tensor(f"t{uid[0]}_{n}", list(s), d))

        def MM(out, lhsT, rhs, start, stop, reads, writes):
            S.op("pe", lambda e: e.matmul(out, lhsT=lhsT, rhs=rhs, start=start, stop=stop,
                                          skip_group_check=True), reads, writes)

        def TR(out, in_, ident_ap, reads, writes):
            S.op("pe", lambda e: e.transpose(out=out, in_=in_, identity=ident_ap), reads, writes)

        def ACT(out, in_, func, reads, writes, **kw):
            S.op("act", lambda e: e.activation(out=out, in_=in_, func=func, **kw), reads, writes)

        def TT(eng, out, in0, in1, op, reads, writes):
            S.op(eng, lambda e: e.tensor_tensor(out=out, in0=in0, in1=in1, op=op), reads, writes)

        def TS(eng, out, in0, s1, op0, reads, writes, s2=None, op1=None):
            if op1 is None:
                S.op(eng, lambda e: e.tensor_scalar(out=out, in0=in0, scalar1=s1, scalar2=None, op0=op0),
                     reads, writes)
            else:
                S.op(eng, lambda e: e.tensor_scalar(out=out, in0=in0, scalar1=s1, scalar2=s2, op0=op0, op1=op1),
                     reads, writes)

        def STT(out, in0, scalar, in1, op0, op1, reads, writes):
            S.op("dve", lambda e: e.scalar_tensor_tensor(out=out, in0=in0, scalar=scalar, in1=in1,
                                                         op0=op0, op1=op1), reads, writes)

        def CP(eng, out, in_, reads, writes):
            if eng == "act":
                ACT(out, in_, AF.Copy, reads, writes)
            else:
                S.op(eng, lambda e: e.tensor_copy(out=out, in_=in_), reads, writes)

        def RED(out, in_, reads, writes, op=ALU.add):
            S.op("dve", lambda e: e.tensor_reduce(out=out, in_=in_, axis=AX.X, op=op), reads, writes)

        def RECIP(out, in_, reads, writes):
            S.op("dve", lambda e: e.reciprocal(out=out, in_=in_), reads, writes)

        def MSET(eng, ap, val, writes):
            S.op(eng, lambda e: e.memset(ap, val), (), writes)

        def DMA(key, out, in_, reads, writes, queue="sp"):
            S.dma(key, lambda e: e.dma_start(out=out, in_=in_), reads, writes, queue=queue)

        PS = [st.enter_context(nc.psum_tensor(f"ps{i}", [128, 512], F32)) for i in range(8)]
        PSB = [p[:].bitcast(BF16) for p in PS]
        PN = [f"ps{i}" for i in range(8)]

        identf = sbt(st, "identf", [128, 128], F32)
        ident = sbt(st, "ident", [128, 128], BF16)
        negm = sbt(st, "negm", [128, 128], BF16)
        negmf = sbt(st, "negmf", [128, 128], F32)
        zeros_bf = sbt(st, "zeros_bf", [128, 512], BF16)
        zero1 = sbt(st, "zero1", [128, 1], F32)
        epsc = sbt(st, "epsc", [128, 1], F32)
        g_ffn = sbt(st, "g_ffn", [128, 1024], F32)
        iota_i = sbt(st, "iota_i", [128, 128], I32)
        iota_f = sbt(st, "iota_f", [128, 128], F32)
        thr_i = sbt(st, "thr_i", [128, 15], I32)
        thr_f = sbt(st, "thr_f", [128, 15], F32)
        ast = ExitStack()
        g_attn = sbt(ast, "g_attn", [128, 1024], F32)
        g_cq = sbt(ast, "g_cq", [128, 384], F32)
        g_ckv = sbt(ast, "g_ckv", [128, 256], F32)
        g_q = sbt(ast, "g_q", [128, 96], F32)
        g_k = sbt(ast, "g_k", [128, 96], F32)
        g_sbo = sbt(ast, "g_sbo", [128, 512], F32)
        g_mlao = sbt(ast, "g_mlao", [128, 512], F32)
        cs = sbt(ast, "cs", [128, NSEQ * NB, 32], F32)
        vsb_all = sbt(ast, "vsb_all", [128, NB, 512], BF16)
        vaug_all = sbt(ast, "vaug_all", [128, NB, 8, 65], BF16)
        stg = [sbt(ast, f"stg{i}", [128, 1024], BF16) for i in range(4)]
        S.op("pool", lambda e: e.iota(iota_i[:], pattern=[[1, 128]], base=0, channel_multiplier=0), (), ["iota_i"])
        CP("dve", iota_f[:], iota_i[:], ["iota_i"], ["iota_f"])
        S.op("pool", lambda e: e.iota(thr_i[:], pattern=[[16, 15]], base=16, channel_multiplier=0), (), ["thr_i"])
        CP("dve", thr_f[:], thr_i[:], ["thr_i"], ["thr_f"])

        MSET("pool", identf[:], 0.0, ["identf"])
        S.op("pool", lambda e: e.affine_select(out=identf[:], in_=identf[:], pattern=[[-1, 128]],
                                               compare_op=ALU.not_equal, fill=1.0, base=0,
                                               channel_multiplier=1), ["identf"], ["identf"])
        CP("dve", ident[:], identf[:], ["identf"], ["ident"])
        MSET("pool", negmf[:], 0.0, ["negmf"])
        S.op("pool", lambda e: e.affine_select(out=negmf[:], in_=negmf[:], pattern=[[-1, 128]],
                                               compare_op=ALU.is_gt, fill=-BIG, base=0,
                                               channel_multiplier=1), ["negmf"], ["negmf"])
        CP("dve", negm[:], negmf[:], ["negmf"], ["negm"])
        MSET("dve", zeros_bf[:], 0.0, ["zeros_bf"])
        MSET("dve", zero1[:], 0.0, ["zero1"])
        MSET("dve", epsc[:], EPS, ["epsc"])
        MSET("pool", vaug_all[:, :, :, 64:65], 1.0, ["vaug_all"])
        for nm, t, src, n in [("g_attn", g_attn, g_attn_d, 1024), ("g_ffn", g_ffn, g_ffn_d, 1024),
                              ("g_cq", g_cq, g_cq_d, 384), ("g_ckv", g_ckv, g_ckv_d, 256),
                              ("g_q", g_q, g_q_d, 96), ("g_k", g_k, g_k_d, 96),
                              ("g_sbo", g_sbo, g_sbo_d, 512), ("g_mlao", g_mlao, g_mlao_d, 512)]:
            DMA(nm, t[:], src.broadcast_to([128, n]), [], [nm])

        with ExitStack() as ph:
            NJ = NSEQ * NB
            posi = sbt(ph, "posi", [128, NJ], I32)
            posf = sbt(ph, "posf", [128, NJ], F32)
            invf = sbt(ph, "invf", [128, 16], F32)
            ang = sbt(ph, "ang", [128, NJ, 16], F32)
            kk = sbt(ph, "kk", [128, NJ, 16], F32)
            kki = sbt(ph, "kki", [128, NJ, 16], I32)
            yy = sbt(ph, "yy", [128, NJ, 16], F32)
            msk = sbt(ph, "msk", [128, NJ, 16], F32)
            DMA("posi", posi[:], pos_d, [], ["posi"])
            DMA("invf", invf[:], invf_d.broadcast_to([128, 16]), [], ["invf"])
            CP("dve", posf[:], posi[:], ["posi"], ["posf"])
            TT("dve", ang[:], posf[:].unsqueeze(2).broadcast_to([128, NJ, 16]),
               invf[:].unsqueeze(1).broadcast_to([128, NJ, 16]), ALU.mult, ["posf", "invf"], ["ang"])
            C1 = 6.28125
            C2 = 2.0 * math.pi - C1
            for which, shift in ((1, 0.0), (0, math.pi / 2)):
                TS("dve", yy[:], ang[:], shift, ALU.add, ["ang"], ["yy"])
                TS("dve", kk[:], yy[:], 1.0 / (2.0 * math.pi), ALU.mult, ["yy"], ["kk"])
                CP("dve", kki[:], kk[:], ["kk"], ["kki"])
                CP("dve", kk[:], kki[:], ["kki"], ["kk"])
                STT(yy[:], kk[:], -C1, yy[:], ALU.mult, ALU.add, ["kk", "yy"], ["yy"])
                STT(yy[:], kk[:], -C2, yy[:], ALU.mult, ALU.add, ["kk", "yy"], ["yy"])
                TS("dve", msk[:], yy[:], math.pi, ALU.is_gt, ["yy"], ["msk"])
                STT(yy[:], msk[:], -2.0 * math.pi, yy[:], ALU.mult, ALU.add, ["msk", "yy"], ["yy"])
                TS("dve", msk[:], yy[:], -math.pi, ALU.is_lt, ["yy"], ["msk"])
                STT(yy[:], msk[:], 2.0 * math.pi, yy[:], ALU.mult, ALU.add, ["msk", "yy"], ["yy"])
                TS("dve", yy[:], yy[:], math.pi, ALU.min, ["yy"], ["yy"], s2=-math.pi, op1=ALU.max)
                ACT(cs[:, :, which * 16:(which + 1) * 16], yy[:], AF.Sin, ["yy"], ["cs"])
            S.barrier()

        for b in range(NSEQ):
            with ExitStack() as ph:
                w_in = sbt(ph, "w_in_s", [128, 8, 2208], BF16)
                w_uq = sbt(ph, "w_uq_s", [128, 3, 768], BF16)
                w_ukv = sbt(ph, "w_ukv_s", [128, 2, 1024], BF16)
                for ck in range(8):
                    DMA("w_in", w_in[:, ck, 0:1104], w_in_d[ck * 128:(ck + 1) * 128, 0:1104], [], ["w_in"], queue="pool")
                    DMA("w_in", w_in[:, ck, 1104:2208], w_in_d[ck * 128:(ck + 1) * 128, 1104:2208], [], ["w_in"], queue="pool")
                for ck in range(3):
                    DMA("w_uq", w_uq[:, ck, :], w_uq_d[ck * 128:(ck + 1) * 128, :], [], ["w_uq"], queue="pool")
                for ck in range(2):
                    DMA("w_ukv", w_ukv[:, ck, :], w_ukv_d[ck * 128:(ck + 1) * 128, :], [], ["w_ukv"], queue="pool")
                if b == 0 and do_peer:
                    items = []
                    for i in range(128):
                        items.append((uT_d[:, i * 128:(i + 1) * 128].rearrange("(ck p) e -> p ck e", p=128), u_bf_d[i], True))
                        items.append((v_d[i * 128:(i + 1) * 128, :], v_bf_d[i], False))
                    for n in range(len(items) + 2):
                        if n < len(items):
                            src_ap, _, is_u = items[n]
                            sl = n % 4
                            dst_ap = stg[sl][:].rearrange("p (ck e) -> p ck e", ck=8) if is_u else stg[sl][:]
                            DMA(f"stgc{sl}", dst_ap, src_ap, [], [f"stg{sl}"], queue="pool")
                        m = n - 2
                        if m >= 0:
                            sl = m % 4
                            DMA(f"stgs{sl}", items[m][1], stg[sl][:], [f"stg{sl}"], [], queue="pool")
                xts = [sbt(ph, f"xt{i}", [128, 1024], F32) for i in range(2)]
                junk = sbt(ph, "junk", [128, 1024], F32)
                hns = [sbt(ph, f"hn{i}", [128, 1024], BF16) for i in range(2)]
                hTs = [sbt(ph, f"hT{i}", [128, 8, 128], BF16) for i in range(2)]
                sttfs = [sbt(ph, f"sttf{i}", [128, 4], F32) for i in range(2)]
                stt = sbt(ph, "stt", [128, 16], F32)
                junkb = sbt(ph, "junkb", [128, 384], F32)
                st8 = sbt(ph, "st8", [128, 32], F32)
                qk_e = [sbt(ph, f"qk_e{i}", [128, 4, 128], BF16) for i in range(2)]
                lats = [sbt(ph, f"lat{i}", [128, 672], F32) for i in range(2)]
                latn = sbt(ph, "latn", [128, 640], BF16)
                latT = sbt(ph, "latT", [128, 5, 128], BF16)
                q_sb = sbt(ph, "q_sb", [128, 8, 96], F32)
                kv_sb = sbt(ph, "kv_sb", [128, 8, 128], F32)
                tmpq = sbt(ph, "tmpq", [128, 8, 96], F32)
                kn = sbt(ph, "kn", [128, 8, 96], F32)
                rt = [sbt(ph, f"rt{i}", [128, 8, 16], F32) for i in range(4)]
                qr = sbt(ph, "qr", [128, 8, 96], BF16)
                kr = sbt(ph, "kr", [128, 8, 96], BF16)
                qT_t = sbt(ph, "qT_t", [96, 8, 128], BF16)
                kT_t = sbt(ph, "kT_t", [96, 8, 128], BF16)

                def rope(src, dst, j, rd, wr):
                    x1 = src[:, :, 64:80]
                    x2 = src[:, :, 80:96]
                    cosb = cs[:, j, 0:16].unsqueeze(1).broadcast_to([128, 8, 16])
                    sinb = cs[:, j, 16:32].unsqueeze(1).broadcast_to([128, 8, 16])
                    TT("dve", rt[0][:], x1, cosb, ALU.mult, [rd, "cs"], ["rt0"])
                    TT("dve", rt[1][:], x2, sinb, ALU.mult, [rd, "cs"], ["rt1"])
                    TT("dve", rt[2][:], x1, sinb, ALU.mult, [rd, "cs"], ["rt2"])
                    TT("dve", rt[3][:], x2, cosb, ALU.mult, [rd, "cs"], ["rt3"])
                    TT("dve", dst[:, :, 64:80], rt[0][:], rt[1][:], ALU.subtract, ["rt0", "rt1"], [wr])
                    TT("dve", dst[:, :, 80:96], rt[2][:], rt[3][:], ALU.add, ["rt2", "rt3"], [wr])

                def blk(tb):
                    j = b * NB + tb
                    u = tb % 2
                    hn = hns[u]
                    hT = hTs[u]
                    lat = lats[u]
                    sttf = sttfs[u]
                    hn_n, hT_n, lat_n, sttf_n = f"hn{u}", f"hT{u}", f"lat{u}", f"sttf{u}"
                    t0 = tb * 128
                    xt = xts[tb % 2]
                    xn = f"xt{tb % 2}"
                    ACT(junk[:], xt[:], AF.Square, [xn], ["junk", sttf_n], accum_out=sttf[:, 0:1])
                    ACT(sttf[:, 1:2], sttf[:, 0:1], AF.Sqrt, [sttf_n, "epsc"], [sttf_n], scale=1.0 / 1024, bias=epsc[:])
                    RECIP(sttf[:, 2:3], sttf[:, 1:2], [sttf_n], [sttf_n])
                    STT(hn[:], xt[:], sttf[:, 2:3], g_attn[:], ALU.mult, ALU.mult, [xn, sttf_n, "g_attn"], [hn_n])
                    if tb + 2 < NB:
                        DMA(xn, xt[:], x_d[b, t0 + 256:t0 + 384, :], [], [xn])
                    pT = PSB[0].rearrange("p (a t) -> p a t", a=8)
                    for ck in range(8):
                        TR(pT[:, ck, :], hn[:, ck * 128:(ck + 1) * 128], ident[:], [hn_n, "ident"], [PN[0]])
                    CP("act", hT[:], pT, [PN[0]], [hT_n])
                    for qi, (bank, col0, dst) in enumerate(((1, 0, qt_sb_d), (2, 512, kt_sb_d))):
                        pq = PS[bank][:].rearrange("p (a t) -> p a t", a=4)
                        for p4 in range(4):
                            for ck in range(8):
                                MM(pq[:, p4, :], w_in[:, ck, col0 + p4 * 128:col0 + (p4 + 1) * 128], hT[:, ck, :],
                                   ck == 0, ck == 7, ["w_in", hT_n], [PN[bank]])
                        CP("dve" if qi == 0 else "act", qk_e[qi][:], pq, [PN[bank]], [f"qk_e{qi}"])
                        DMA(f"qk_e{qi}", dst[b].rearrange("(p e) d t -> (e d) p t", e=2)[:, :, t0:t0 + 128],
                            qk_e[qi][:], [f"qk_e{qi}"], [f"qkscr{qi}"])
                    for ck in range(8):
                        MM(PS[3][:, 0:512], hT[:, ck, :], w_in[:, ck, 1024:1536], ck == 0, ck == 7,
                           ["w_in", hT_n], [PN[3]])
                    CP("act", vsb_all[:, tb, :], PS[3][:, 0:512], [PN[3]], ["vsb_all"])
                    for ck in range(8):
                        MM(PS[4][:, 0:512], hT[:, ck, :], w_in[:, ck, 1536:2048], ck == 0, ck == 7,
                           ["w_in", hT_n], [PN[4]])
                    for ck in range(8):
                        MM(PS[5][:, 0:160], hT[:, ck, :], w_in[:, ck, 2048:2208], ck == 0, ck == 7,
                           ["w_in", hT_n], [PN[5]])
                    CP("dve", lat[:, 0:512], PS[4][:, 0:512], [PN[4]], [lat_n])
                    CP("act", lat[:, 512:672], PS[5][:, 0:160], [PN[5]], [lat_n])
                    yield
                    pT6 = PSB[6].rearrange("p (a t) -> p a t", a=8)
                    ACT(junkb[:, 0:384], lat[:, 0:384], AF.Square, [lat_n], ["junkb", "stt"], accum_out=stt[:, 3:4])
                    ACT(junkb[:, 0:256], lat[:, 384:640], AF.Square, [lat_n], ["junkb", "stt"], accum_out=stt[:, 4:5])
                    ACT(junkb[:, 0:32], lat[:, 640:672], AF.Square, [lat_n], ["junkb", "stt"], accum_out=stt[:, 11:12])
                    TS("dve", stt[:, 5:6], stt[:, 3:4], 1.0 / 384, ALU.mult, ["stt"], ["stt"], s2=EPS, op1=ALU.add)
                    TS("dve", stt[:, 6:7], stt[:, 4:5], 1.0 / 256, ALU.mult, ["stt"], ["stt"], s2=EPS, op1=ALU.add)
                    ACT(stt[:, 7:9], stt[:, 5:7], AF.Sqrt, ["stt"], ["stt"])
                    RECIP(stt[:, 9:11], stt[:, 7:9], ["stt"], ["stt"])
                    STT(latn[:, 0:384], lat[:, 0:384], stt[:, 9:10], g_cq[:], ALU.mult, ALU.mult,
                        [lat_n, "stt", "g_cq"], ["latn"])
                    STT(latn[:, 384:640], lat[:, 384:640], stt[:, 10:11], g_ckv[:], ALU.mult, ALU.mult,
                        [lat_n, "stt", "g_ckv"], ["latn"])
                    for ck in range(5):
                        TR(pT6[:, ck, :], latn[:, ck * 128:(ck + 1) * 128], ident[:], ["latn", "ident"], [PN[6]])
                    CP("act", latT[:], pT6[:, 0:5, :], [PN[6]], ["latT"])
                    for ck in range(3):
                        MM(PS[6][:, 0:512], latT[:, ck, :], w_uq[:, ck, 0:512], ck == 0, ck == 2, ["latT", "w_uq"], [PN[6]])
                    for ck in range(3):
                        MM(PS[7][:, 0:256], latT[:, ck, :], w_uq[:, ck, 512:768], ck == 0, ck == 2, ["latT", "w_uq"], [PN[7]])
                    qf = q_sb[:].rearrange("p h d -> p (h d)")
                    CP("act", qf[:, 0:512], PS[6][:, 0:512], [PN[6]], ["q_sb"])
                    CP("dve", qf[:, 512:768], PS[7][:, 0:256], [PN[7]], ["q_sb"])
                    for ck in range(2):
                        MM(PS[6][:, 0:512], latT[:, 3 + ck, :], w_ukv[:, ck, 0:512], ck == 0, ck == 1, ["latT", "w_ukv"], [PN[6]])
                    for ck in range(2):
                        MM(PS[7][:, 0:512], latT[:, 3 + ck, :], w_ukv[:, ck, 512:1024], ck == 0, ck == 1, ["latT", "w_ukv"], [PN[7]])
                    kvf = kv_sb[:].rearrange("p h d -> p (h d)")
                    CP("act", kvf[:, 0:512], PS[6][:, 0:512], [PN[6]], ["kv_sb"])
                    CP("dve", kvf[:, 512:1024], PS[7][:, 0:512], [PN[7]], ["kv_sb"])
                    TT("dve", tmpq[:], q_sb[:], q_sb[:], ALU.mult, ["q_sb"], ["tmpq"])
                    RED(st8[:, 0:8], tmpq[:], ["tmpq"], ["st8"])
                    TS("dve", st8[:, 0:8], st8[:, 0:8], 1.0 / 96, ALU.mult, ["st8"], ["st8"], s2=EPS, op1=ALU.add)
                    ACT(st8[:, 8:16], st8[:, 0:8], AF.Sqrt, ["st8"], ["st8"])
                    RECIP(st8[:, 0:8], st8[:, 8:16], ["st8"], ["st8"])
                    TT("dve", tmpq[:], q_sb[:], st8[:, 0:8].unsqueeze(2).broadcast_to([128, 8, 96]), ALU.mult,
                       ["q_sb", "st8"], ["tmpq"])
                    TT("dve", tmpq[:], tmpq[:], g_q[:].unsqueeze(1).broadcast_to([128, 8, 96]), ALU.mult,
                       ["tmpq", "g_q"], ["tmpq"])
                    CP("dve", qr[:, :, 0:64], tmpq[:, :, 0:64], ["tmpq"], ["qr"])
                    rope(tmpq, qr, j, "tmpq", "qr")
                    pT2 = PSB[6].rearrange("p (a t) -> p a t", a=8)
                    for h in range(8):
                        TR(pT2[0:96, h, :], qr[:, h, :], ident[:], ["qr", "ident"], [PN[6]])
                    CP("act", qT_t[:], pT2[0:96, :, :], [PN[6]], ["qT_t"])
                    DMA("qT_t", qt_ml_d[b].rearrange("h d t -> d h t")[:, :, t0:t0 + 128], qT_t[:], ["qT_t"], ["qtml"])
                    TT("dve", kn[:, :, 0:64], kv_sb[:, :, 0:64], kv_sb[:, :, 0:64], ALU.mult, ["kv_sb"], ["kn"])
                    RED(st8[:, 16:24], kn[:, :, 0:64], ["kn"], ["st8"])
                    TS("dve", st8[:, 16:24], st8[:, 16:24], stt[:, 11:12], ALU.add, ["st8", "stt"], ["st8"],
                       s2=1.0 / 96, op1=ALU.mult)
                    TS("dve", st8[:, 16:24], st8[:, 16:24], EPS, ALU.add, ["st8"], ["st8"])
                    ACT(st8[:, 24:32], st8[:, 16:24], AF.Sqrt, ["st8"], ["st8"])
                    RECIP(st8[:, 16:24], st8[:, 24:32], ["st8"], ["st8"])
                    rk = st8[:, 16:24]
                    TT("dve", kn[:, :, 0:64], kv_sb[:, :, 0:64], rk.unsqueeze(2).broadcast_to([128, 8, 64]), ALU.mult,
                       ["kv_sb", "st8"], ["kn"])
                    TT("dve", kn[:, :, 64:96], lat[:, 640:672].unsqueeze(1).broadcast_to([128, 8, 32]),
                       rk.unsqueeze(2).broadcast_to([128, 8, 32]), ALU.mult, [lat_n, "st8"], ["kn"])
                    TT("dve", kn[:], kn[:], g_k[:].unsqueeze(1).broadcast_to([128, 8, 96]), ALU.mult,
                       ["kn", "g_k"], ["kn"])
                    CP("dve", kr[:, :, 0:64], kn[:, :, 0:64], ["kn"], ["kr"])
                    rope(kn, kr, j, "kn", "kr")
                    CP("act", vaug_all[:, tb, :, 0:64], kv_sb[:, :, 64:128], ["kv_sb"], ["vaug_all"])
                    pT3 = PSB[7].rearrange("p (a t) -> p a t", a=8)
                    for h in range(8):
                        TR(pT3[0:96, h, :], kr[:, h, :], ident[:], ["kr", "ident"], [PN[7]])
                    CP("dve", kT_t[:], pT3[0:96, :, :], [PN[7]], ["kT_t"])
                    DMA("kT_t", kt_ml_d[b].rearrange("h d t -> d h t")[:, :, t0:t0 + 128], kT_t[:], ["kT_t"], ["ktml"])

                DMA("xt0", xts[0][:], x_d[b, 0:128, :], [], ["xt0"])
                DMA("xt1", xts[1][:], x_d[b, 128:256, :], [], ["xt1"])
                gens = [blk(tb) for tb in range(NB)]

                def exhaust(g_):
                    for _ in g_:
                        pass

                S.emit_interleaved(S.record(lambda: next(gens[0])), [])
                for tb in range(NB):
                    fl = S.record(lambda: next(gens[tb + 1])) if tb + 1 < NB else []
                    bl = S.record(lambda: exhaust(gens[tb]))
                    S.emit_interleaved(fl, bl)
                S.barrier()

            with ExitStack() as ph:
                qh = [sbt(ph, f"qh{i}", [64, S_LEN], BF16) for i in range(2)]
                kh = [sbt(ph, f"kh{i}", [64, S_LEN], BF16) for i in range(2)]
                NS = 8
                fch = [sbt(ph, f"fch{i}", [128, 512], F32) for i in range(NS)]
                Pch = [sbt(ph, f"Pch{i}", [128, 513], F32) for i in range(NS)]
                Ach = [sbt(ph, f"Ach{i}", [128, 512], BF16) for i in range(NS)]
                ATc = [sbt(ph, f"ATc{i}", [128, 4, 128], BF16) for i in range(NS)]
                osb = [sbt(ph, f"osb{i}", [128, 64], F32) for i in range(2)]
                DMA("qh0", qh[0][:], qt_sb_d[b, 0], ["qkscr0"], ["qh0"])
                DMA("kh0", kh[0][:], kt_sb_d[b, 0], ["qkscr1"], ["kh0"])
                chunks = []
                itn = 0
                for h in range(8):
                    for qb0 in range(0, NB, 2):
                        rows = []
                        for qb in (qb0, qb0 + 1):
                            L = (qb + 1) * 128
                            nch = (L + 511) // 512
                            rows.append([(h, qb, c, min(512, L - c * 512), c == nch - 1, c == 0, itn + (qb - qb0))
                                         for c in range(nch - 1, -1, -1)])
                        itn += 2
                        last_idx = [None, None]
                        for k_ in range(max(len(rows[0]), len(rows[1]))):
                            for r_ in range(2):
                                if k_ < len(rows[r_]):
                                    chunks.append(rows[r_][k_] + (last_idx[r_],))
                                    last_idx[r_] = len(chunks) - 1
                seen_heads = set()

                def stA(n):
                    h, qb, c, w, first, last, it, prev = chunks[n]
                    hs = h % 2
                    if h not in seen_heads and h + 1 < 8:
                        seen_heads.add(h)
                        ns = (h + 1) % 2
                        DMA(f"qh{ns}", qh[ns][:], qt_sb_d[b, h + 1], ["qkscr0"], [f"qh{ns}"])
                        DMA(f"kh{ns}", kh[ns][:], kt_sb_d[b, h + 1], ["qkscr1"], [f"kh{ns}"])
                    bank = n % 3
                    sl = n % NS
                    MM(PS[bank][:, 0:w], qh[hs][:, qb * 128:(qb + 1) * 128], kh[hs][:, c * 512:c * 512 + w],
                       True, not first, [f"qh{hs}", f"kh{hs}"], [PN[bank]])
                    if first:
                        MM(PS[bank][:, w - 128:w], ident[:], negm[:], False, True, ["ident", "negm"], [PN[bank]])
                    ACT(fch[sl][:, 0:w], PS[bank][:, 0:w], AF.Sigmoid, [PN[bank]], [f"fch{sl}"], scale=-0.125)

                def stB(n):
                    h, qb, c, w, first, last, it, prev = chunks[n]
                    sl = n % NS
                    P = Pch[sl]
                    if first:
                        S.op("dve", lambda e, P=P, w=w, sl=sl: e.tensor_tensor_scan(
                            out=P[:, w - 1::-1], data0=fch[sl][:, w - 1::-1], data1=zero1[:].broadcast_to([128, w]),
                            initial=1.0, op0=ALU.mult, op1=ALU.add),
                            [f"fch{sl}", "zero1"], [f"Pch{sl}"])
                        MSET("pool", P[:, w:w + 1], 1.0, [f"Pcar{sl}"])
                    else:
                        pl = prev % NS
                        S.op("dve", lambda e, P=P, w=w, sl=sl, pl=pl: e.tensor_tensor_scan(
                            out=P[:, w - 1::-1], data0=fch[sl][:, w - 1::-1], data1=zero1[:].broadcast_to([128, w]),
                            initial=Pch[pl][:, 0:1], op0=ALU.mult, op1=ALU.add),
                            [f"fch{sl}", "zero1", f"Pch{pl}"], [f"Pch{sl}"])
                        CP("pool", P[:, w:w + 1], Pch[pl][:, 0:1], [f"Pch{pl}"], [f"Pcar{sl}"])
                    if w == 512:
                        TT("pool", Ach[sl][:, 0:352], P[:, 1:353], P[:, 0:352], ALU.subtract, [f"Pch{sl}"],
                           [f"Ach{sl}"])
                        TT("dve", Ach[sl][:, 352:512], P[:, 353:513], P[:, 352:512], ALU.subtract,
                           [f"Pch{sl}", f"Pcar{sl}"], [f"AchB{sl}"])
                    else:
                        TT("pool", Ach[sl][:, 0:w], P[:, 1:w + 1], P[:, 0:w], ALU.subtract, [f"Pch{sl}", f"Pcar{sl}"],
                           [f"Ach{sl}", f"AchB{sl}"])

                def stC(n):
                    h, qb, c, w, first, last, it, prev = chunks[n]
                    sl = n % NS
                    bank = 3 + n % 2
                    nb_ = w // 128
                    pv = PSB[bank].rearrange("p (a t) -> p a t", a=8)
                    for k_ in range(nb_):
                        TR(pv[:, k_, :], Ach[sl][:, k_ * 128:(k_ + 1) * 128], ident[:], [f"Ach{sl}", f"AchB{sl}", "ident"], [PN[bank]])
                    CP("act", ATc[sl][:, 0:nb_, :], pv[:, 0:nb_, :], [PN[bank]], [f"ATc{sl}"])

                def stD(n):
                    h, qb, c, w, first, last, it, prev = chunks[n]
                    sl = n % NS
                    ob = 5 + it % 2
                    nb_ = w // 128
                    for k_ in range(nb_):
                        kbk = c * 4 + k_
                        MM(PS[ob][:, 0:64], ATc[sl][:, k_, :], vsb_all[:, kbk, h * 64:(h + 1) * 64],
                           first and k_ == 0, last and k_ == nb_ - 1, [f"ATc{sl}", "vsb_all"], [PN[ob]])
                    if last:
                        ps_ = it % 2
                        CP("dve", osb[ps_][:], PS[ob][:, 0:64], [PN[ob]], [f"osb{ps_}"])
                        DMA(f"osb{ps_}", mixraw_d[b, qb * 128:(qb + 1) * 128, h * 64:(h + 1) * 64], osb[ps_][:],
                            [f"osb{ps_}"], [])

                NCH = len(chunks)
                for t in range(NCH + 4):
                    if t < NCH:
                        stA(t)
                    if 0 <= t - 1 < NCH:
                        stB(t - 1)
                    if 0 <= t - 3 < NCH:
                        stC(t - 3)
                    if 0 <= t - 4 < NCH:
                        stD(t - 4)
                S.barrier()

            with ExitStack() as ph:
                qh = [sbt(ph, f"mqh{i}", [96, S_LEN], BF16) for i in range(2)]
                kh = [sbt(ph, f"mkh{i}", [96, S_LEN], BF16) for i in range(2)]
                PT = [sbt(ph, f"PT{i}", [128, 512], BF16) for i in range(3)]
                om = [sbt(ph, f"om{i}", [128, 4, 64], F32) for i in range(2)]
                rec = [sbt(ph, f"rec{i}", [128, 4], F32) for i in range(2)]
                sc_mla = 96.0 ** -0.5
                it = 0
                gi = 0
                DMA("mqh0", qh[0][:], qt_ml_d[b, 0], ["qtml"], ["mqh0"])
                DMA("mkh0", kh[0][:], kt_ml_d[b, 0], ["ktml"], ["mkh0"])
                steps = [(h, qg, kb) for h in range(8) for qg in range(8) for kb in range(4 * qg + 4)]

                def s_step(n):
                    h, qg, kb = steps[n]
                    hs = h % 2
                    if qg == 0 and kb == 0 and h + 1 < 8:
                        ns = (h + 1) % 2
                        DMA(f"mqh{ns}", qh[ns][:], qt_ml_d[b, h + 1], ["qtml"], [f"mqh{ns}"])
                        DMA(f"mkh{ns}", kh[ns][:], kt_ml_d[b, h + 1], ["ktml"], [f"mkh{ns}"])
                    jj = kb - 4 * qg
                    c0 = 128 * jj if jj > 0 else 0
                    sb_ = n % 3
                    MM(PS[sb_][:, c0:512], kh[hs][:, kb * 128:(kb + 1) * 128],
                       qh[hs][:, qg * 512 + c0:qg * 512 + 512], True, True,
                       [f"mqh{hs}", f"mkh{hs}"], [PN[sb_]])
                    ACT(PT[sb_][:, c0:512], PS[sb_][:, c0:512], AF.Exp, [PN[sb_]], [f"PT{sb_}"], scale=sc_mla)
                    if jj >= 0:
                        MSET("pool", PT[sb_][64:128, c0:c0 + 64], 0.0, [f"PT{sb_}"])

                def pv_step(n):
                    h, qg, kb = steps[n]
                    gi = h * 8 + qg
                    ab = 3 + gi % 2
                    accn = PN[ab]
                    acc = PS[ab][:, 0:260].rearrange("p (q d) -> p q d", q=4)
                    if kb == 0:
                        MM(PS[ab][:, 0:260], zeros_bf[:, 0:128], zeros_bf[:, 0:260], True, False, ["zeros_bf"], [accn])
                    jj = kb - 4 * qg
                    sb_ = n % 3
                    for qq in range(max(jj, 0), 4):
                        MM(acc[:, qq, :], PT[sb_][:, qq * 128:(qq + 1) * 128], vaug_all[:, kb, h, :], False,
                           kb == 4 * qg + qq, [f"PT{sb_}", "vaug_all"], [accn])
                    if kb == 4 * qg + 3:
                        os_ = gi % 2
                        RECIP(rec[os_][:].unsqueeze(2), acc[:, :, 64:65], [accn], [f"rec{os_}"])
                        TT("dve", om[os_][:], acc[:, :, 0:64], rec[os_][:].unsqueeze(2).broadcast_to([128, 4, 64]),
                           ALU.mult, [accn, f"rec{os_}"], [f"om{os_}"])
                        DMA(f"om{os_}",
                            mixraw_d[b, qg * 512:(qg + 1) * 512, 512 + h * 64:512 + (h + 1) * 64].rearrange(
                                "(q p) d -> p q d", p=128),
                            om[os_][:], [f"om{os_}"], ["mixraw"])

                s_step(0)
                for n in range(len(steps)):
                    if n + 1 < len(steps):
                        s_step(n + 1)
                    pv_step(n)
                S.barrier()

            with ExitStack() as ph:
                w_o = sbt(ph, "w_o_s", [128, 8, 1024], BF16)
                for ck in range(8):
                    DMA("w_o", w_o[:, ck, :], w_o_d[ck * 128:(ck + 1) * 128, :], [], ["w_o"], queue="pool")
                xts = [sbt(ph, f"xo{i}", [128, 1024], F32) for i in range(2)]
                mrs = [sbt(ph, f"mr{i}", [128, 1024], F32) for i in range(2)]
                junks = [sbt(ph, f"junk4{i}", [128, 512], F32) for i in range(2)]
                stts = [sbt(ph, f"stt4{i}", [128, 8], F32) for i in range(2)]
                mixns = [sbt(ph, f"mixn{i}", [128, 1024], BF16) for i in range(2)]
                mixTs = [sbt(ph, f"mixT{i}", [128, 8, 128], BF16) for i in range(2)]
                x2s = [sbt(ph, f"x2s{i}", [128, 1024], F32) for i in range(2)]
                for tb in range(NB):
                    t0 = tb * 128
                    s_ = tb % 2
                    junk, stt, mixn, mixT = junks[s_], stts[s_], mixns[s_], mixTs[s_]
                    jn, sn, mn, mTn = f"junk4{s_}", f"stt4{s_}", f"mixn{s_}", f"mixT{s_}"
                    tbank = 0 if s_ == 0 else 5
                    if tb == 0:
                        DMA("xo0", xts[0][:], x_d[b, 0:128, :], [], ["xo0"])
                        DMA("mr0", mrs[0][:], mixraw_d[b, 0:128, :], ["mixraw"], ["mr0"])
                    if tb + 1 < NB:
                        n_ = (tb + 1) % 2
                        t1 = (tb + 1) * 128
                        DMA(f"xo{n_}", xts[n_][:], x_d[b, t1:t1 + 128, :], [], [f"xo{n_}"])
                        DMA(f"mr{n_}", mrs[n_][:], mixraw_d[b, t1:t1 + 128, :], ["mixraw"], [f"mr{n_}"])
                    mr = mrs[s_]
                    ACT(junk[:], mr[:, 0:512], AF.Square, [f"mr{s_}"], [jn, sn], accum_out=stt[:, 0:1])
                    ACT(junk[:], mr[:, 512:1024], AF.Square, [f"mr{s_}"], [jn, sn], accum_out=stt[:, 1:2])
                    ACT(stt[:, 2:4], stt[:, 0:2], AF.Sqrt, [sn, "epsc"], [sn], scale=1.0 / 512, bias=epsc[:])
                    RECIP(stt[:, 4:6], stt[:, 2:4], [sn], [sn])
                    STT(mixn[:, 0:512], mr[:, 0:512], stt[:, 4:5], g_sbo[:], ALU.mult, ALU.mult,
                        [f"mr{s_}", sn, "g_sbo"], [mn])
                    STT(mixn[:, 512:1024], mr[:, 512:1024], stt[:, 5:6], g_mlao[:], ALU.mult, ALU.mult,
                        [f"mr{s_}", sn, "g_mlao"], [mn])
                    pT = PSB[tbank].rearrange("p (a t) -> p a t", a=8)
                    for ck in range(8):
                        TR(pT[:, ck, :], mixn[:, ck * 128:(ck + 1) * 128], ident[:], [mn, "ident"], [PN[tbank]])
                    CP("act", mixT[:], pT, [PN[tbank]], [mTn])
                    for half in range(2):
                        bank = 1 + 2 * s_ + half
                        for ck in range(8):
                            MM(PS[bank][:, 0:512], mixT[:, ck, :], w_o[:, ck, half * 512:(half + 1) * 512], ck == 0,
                               ck == 7, [mTn, "w_o"], [PN[bank]])
                        TT("dve", x2s[s_][:, half * 512:(half + 1) * 512], PS[bank][:, 0:512],
                           xts[s_][:, half * 512:(half + 1) * 512], ALU.add, [PN[bank], f"xo{s_}"], [f"x2s{s_}"])
                    DMA(f"x2s{s_}", out_d[b, t0:t0 + 128, :], x2s[s_][:], [f"x2s{s_}"], ["out"])
                S.barrier()

        ast.close()
        if do_peer:
            TG = 256
            w_pq = sbt(st, "w_pq_s", [128, 8, 2048], BF16)
            skT = sbt(st, "skT_s", [128, 16, 128], BF16)
            for ck in range(8):
                DMA("w_pq", w_pq[:, ck, :], w_pq_d[ck * 128:(ck + 1) * 128, :], [], ["w_pq"], queue="pool")
            DMA("skT", skT[:], skT_d, [], ["skT"], queue="pool")
            iota_b = sbt(st, "iota_b", [128, 128], BF16)
            CP("dve", iota_b[:], iota_f[:], ["iota_f"], ["iota_b"])
            GT = sbt(st, "GT", [128, 128, TG], BF16)
            hn2Ts = [sbt(st, f"hn2T{i}", [128, 8, TG], BF16) for i in range(2)]
            ijgTs = [sbt(st, f"ijgT{i}", [128, 3, TG], F32) for i in range(2)]
            x2r = sbt(st, "x2r", [128, 1024], F32)
            stt = sbt(st, "pstt", [128, 8], F32)
            hn2 = sbt(st, "hn2", [128, 1024], BF16)
            qT = sbt(st, "pqT", [128, 16, 128], BF16)
            sc = sbt(st, "psc", [128, 16, 128], F32)
            sc2 = sbt(st, "psc2", [128, 128], F32)
            tops = sbt(st, "tops", [128, 16, 16], F32)
            topi = sbt(st, "topi", [128, 16, 16], U32)
            topif = sbt(st, "topif", [128, 16, 16], F32)
            cand = sbt(st, "cand", [128, 8, 256], F32)
            cand2 = sbt(st, "cand2", [128, 256], F32)
            best = sbt(st, "best", [128, 8, 16], F32)
            posu = sbt(st, "posu", [128, 8, 16], U32)
            posf = sbt(st, "pposf", [128, 8, 16], F32)
            af = sbt(st, "paf", [128, 8, 16], F32)
            bf = sbt(st, "pbf", [128, 8, 16], F32)
            oh = sbt(st, "poh", [128, 8, 16, 16], F32)
            ijg = sbt(st, "pijg", [128, 3, 8, 16], F32)
            sm = sbt(st, "psm", [128, 16], F32)
            wA = [sbt(st, f"wA{i}", [128, 128], BF16) for i in range(8)]
            wB = [sbt(st, f"wB{i}", [128, 128], BF16) for i in range(8)]
            NU = 7
            ub = [sbt(st, f"ub{i}", [128, 8, 128], BF16) for i in range(NU)]
            vb = [sbt(st, f"vb{i}", [128, 1024], BF16) for i in range(NU)]
            ge = [sbt(st, f"ge{i}", [128, TG], BF16) for i in range(2)]
            cf = [sbt(st, f"cf{i}", [128, TG], BF16) for i in range(2)]
            fin = [sbt(st, f"fin{i}", [128, 1024], F32) for i in range(2)]

            groups = [(b, g) for b in range(NSEQ) for g in range(S_LEN // TG)]

            def routing(gi):
                b, g = groups[gi]
                par = gi % 2
                hn2T = hn2Ts[par]
                hTn = f"hn2T{par}"
                ijgT = ijgTs[par]
                iTn = f"ijgT{par}"
                tg0 = g * TG
                junk = oh[:].rearrange("p h k a -> p (h k a)")[:, 0:1024]
                for tk in range(2):
                    t0 = tg0 + tk * 128
                    DMA("x2r", x2r[:], out_d[b, t0:t0 + 128, :], [f"out{b}_{g}"], ["x2r"])
                    for _ in range(6):
                        yield
                    ACT(junk, x2r[:], AF.Square, ["x2r"], ["poh", "pstt"], accum_out=stt[:, 0:1])
                    ACT(stt[:, 1:2], stt[:, 0:1], AF.Sqrt, ["pstt", "epsc"], ["pstt"], scale=1.0 / 1024, bias=epsc[:])
                    RECIP(stt[:, 2:3], stt[:, 1:2], ["pstt"], ["pstt"])
                    STT(hn2[:], x2r[:], stt[:, 2:3], g_ffn[:], ALU.mult, ALU.mult, ["x2r", "pstt", "g_ffn"], ["hn2"])
                    for _ in range(8):
                        yield
                    pT = PSB[2].rearrange("p (a t) -> p a t", a=8)
                    for ck in range(8):
                        TR(pT[:, ck, :], hn2[:, ck * 128:(ck + 1) * 128], ident[:], ["hn2", "ident"], [PN[2]])
                    CP("act", hn2T[:, :, tk * 128:(tk + 1) * 128], pT, [PN[2]], [hTn])
                    yield
                    for r in range(4):
                        bank = 2 + r % 2
                        pq = PS[bank][:].rearrange("p (a n) -> p a n", a=4)
                        for a4 in range(4):
                            hc = r * 4 + a4
                            for ck in range(8):
                                MM(pq[:, a4, :], w_pq[:, ck, hc * 128:(hc + 1) * 128], hn2T[:, ck, tk * 128:(tk + 1) * 128],
                                   ck == 0, ck == 7, ["w_pq", hTn], [PN[bank]])
                            yield
                        CP("act", qT[:, r * 4:(r + 1) * 4, :], pq, [PN[bank]], ["pqT"])
                    for r in range(4):
                        bank = 2 + r % 2
                        pq = PS[bank][:].rearrange("p (a n) -> p a n", a=4)
                        for a4 in range(4):
                            hc = r * 4 + a4
                            MM(pq[:, a4, :], qT[:, hc, :], skT[:, hc, :], True, True, ["pqT", "skT"], [PN[bank]])
                        CP("act", sc[:, r * 4:(r + 1) * 4, :], pq, [PN[bank]], ["psc"])
                        yield
                    for hc in range(16):
                        S.op("dve", lambda e, hc=hc: e.max(out=tops[:, hc, 0:8], in_=sc[:, hc, :]), ["psc"], ["tops"])
                        S.op("dve", lambda e, hc=hc: e.max_index(out=topi[:, hc, 0:8], in_max=tops[:, hc, 0:8],
                                                                   in_values=sc[:, hc, :]), ["psc", "tops"], ["topi"])
                        S.op("dve", lambda e, hc=hc: e.match_replace(out=sc2[:], in_to_replace=tops[:, hc, 0:8],
                                                                       in_values=sc[:, hc, :], imm_value=-1e30),
                             ["psc", "tops"], ["psc2"])
                        yield
                        S.op("dve", lambda e, hc=hc: e.max(out=tops[:, hc, 8:16], in_=sc2[:]), ["psc2"], ["tops"])
                        S.op("dve", lambda e, hc=hc: e.max_index(out=topi[:, hc, 8:16], in_max=tops[:, hc, 8:16],
                                                                   in_values=sc2[:]), ["psc2", "tops"], ["topi"])
                        yield
                    CP("dve", topif[:], topi[:], ["topi"], ["topif"])
                    tops4 = tops[:].rearrange("p (h c) k -> p h c k", c=2)
                    topif4 = topif[:].rearrange("p (h c) k -> p h c k", c=2)
                    cand4 = cand[:].rearrange("p h (a b) -> p h a b", a=16)
                    TT("dve", cand4, tops4[:, :, 0, :].unsqueeze(3).broadcast_to([128, 8, 16, 16]),
                       tops4[:, :, 1, :].unsqueeze(2).broadcast_to([128, 8, 16, 16]), ALU.add, ["tops"], ["cand"])
                    yield
                    for h in range(8):
                        S.op("dve", lambda e, h=h: e.max(out=best[:, h, 0:8], in_=cand[:, h, :]), ["cand"], ["best"])
                        S.op("dve", lambda e, h=h: e.max_index(out=posu[:, h, 0:8], in_max=best[:, h, 0:8],
                                                                 in_values=cand[:, h, :]), ["cand", "best"], ["posu"])
                        S.op("dve", lambda e, h=h: e.match_replace(out=cand2[:], in_to_replace=best[:, h, 0:8],
                                                                     in_values=cand[:, h, :], imm_value=-1e30),
                             ["cand", "best"], ["cand2"])
                        yield
                        S.op("dve", lambda e, h=h: e.max(out=best[:, h, 8:16], in_=cand2[:]), ["cand2"], ["best"])
                        S.op("dve", lambda e, h=h: e.max_index(out=posu[:, h, 8:16], in_max=best[:, h, 8:16],
                                                                 in_values=cand2[:]), ["cand2", "best"], ["posu"])
                        yield
                    gsm = ijg[:, 2, :, :]
                    TT("dve", gsm, best[:], best[:, :, 0:1].broadcast_to([128, 8, 16]), ALU.subtract, ["best"], ["pijg"])
                    for _ in range(16):
                        yield
                    ACT(gsm, gsm, AF.Exp, ["pijg"], ["pijg"])
                    RED(sm[:, 0:8], gsm, ["pijg"], ["psm"])
                    RECIP(sm[:, 8:16], sm[:, 0:8], ["psm"], ["psm"])
                    TT("dve", gsm, gsm, sm[:, 8:16].unsqueeze(2).broadcast_to([128, 8, 16]), ALU.mult, ["pijg", "psm"], ["pijg"])
                    yield
                    CP("dve", posf[:], posu[:], ["posu"], ["pposf"])
                    oh15 = oh[:].rearrange("p h k a -> p (h k) a")[:, :, 0:15]
                    TT("dve", oh15, posf[:].rearrange("p h k -> p (h k)").unsqueeze(2).broadcast_to([128, 128, 15]),
                       thr_f[:].unsqueeze(1).broadcast_to([128, 128, 15]), ALU.is_ge, ["pposf", "thr_f"], ["poh"])
                    RED(af[:].rearrange("p h k -> p (h k)"), oh15, ["poh"], ["paf"])
                    STT(bf[:], af[:], -16.0, posf[:], ALU.mult, ALU.add, ["paf", "pposf"], ["pbf"])
                    yield
                    for w_, src in ((0, af), (1, bf)):
                        TT("dve", oh[:], src[:].unsqueeze(3).broadcast_to([128, 8, 16, 16]),
                           iota_f[:, 0:16].unsqueeze(1).unsqueeze(1).broadcast_to([128, 8, 16, 16]), ALU.is_equal,
                           ["paf", "pbf", "iota_f"], ["poh"])
                        TT("dve", oh[:], oh[:], topif4[:, :, w_, :].unsqueeze(2).broadcast_to([128, 8, 16, 16]), ALU.mult,
                           ["poh", "topif"], ["poh"])
                        RED(ijg[:, w_, :, :], oh[:], ["poh"], ["pijg"])
                        yield
                    for _ in range(18):
                        yield
                    for w_ in range(3):
                        TR(PS[3][:, w_ * 128:(w_ + 1) * 128], ijg[:, w_, :, :].rearrange("p h k -> p (h k)"), identf[:],
                           ["pijg", "identf"], [PN[3]])
                    CP("act", ijgT[:, :, tk * 128:(tk + 1) * 128], PS[3][:, 0:384].rearrange("p (w t) -> p w t", w=3),
                       [PN[3]], [iTn])
                    yield

            def drain(gen):
                if gen is not None:
                    for _ in gen:
                        pass

            def step(gen, n):
                if gen is None:
                    return None
                for _ in range(n):
                    try:
                        next(gen)
                    except StopIteration:
                        return None
                return gen

            drain(routing(0))
            for gi, (b, g) in enumerate(groups):
                par = gi % 2
                hn2T = hn2Ts[par]
                hTn = f"hn2T{par}"
                ijgT = ijgTs[par]
                iTn = f"ijgT{par}"
                tg0 = g * TG
                for t in range(TG):
                    s4 = t % 4
                    s8 = t % 8
                    if s4 == 0:
                        grp = [f"wA{s8 + q_}" for q_ in range(4)] + [f"wB{s8 + q_}" for q_ in range(4)]
                        wra, wrb = grp, [f"wB{s8}"]
                    else:
                        wra, wrb = [f"wA{s8}"], [f"wB{s8}"]
                    TS("dve", wA[s8][:], iota_b[:], ijgT[:, 0, t:t + 1], ALU.is_equal, ["iota_b", iTn], wra)
                    TS("dve", wB[s8][:], iota_b[:], ijgT[:, 1, t:t + 1], ALU.is_equal, ["iota_b", iTn], wrb,
                       s2=ijgT[:, 2, t:t + 1], op1=ALU.mult)
                    bank = 2 + (t // 4) % 2
                    MM(PS[bank][:, s4 * 128:(s4 + 1) * 128], wB[s8][:], wA[s8][:], True, True,
                       [f"wA{s8}", f"wB{s8}"], [PN[bank]])
                    if s4 == 3:
                        CP("act", GT[:, :, t - 3:t + 1], PS[bank][:].rearrange("p (t i) -> p i t", t=4), [PN[bank]], ["GT"])
                for tk in range(2):
                    t0 = tg0 + tk * 128
                    DMA(f"finl{tk}", fin[tk][:], out_d[b, t0:t0 + 128, :], [f"out{b}_{g}"], [f"fin{tk}"])
                rgen = routing(gi + 1) if gi + 1 < len(groups) else None

                def loads(i):
                    s_ = i % NU
                    DMA(f"ub{s_}", ub[s_][:].rearrange("p a e -> p (a e)"), u_bf_d[i], [], [f"ub{s_}"])
                    DMA(f"vb{s_}", vb[s_][:], v_bf_d[i], [], [f"vb{s_}"])

                def stage1(i):
                    s_ = i % NU
                    pb = i % 2
                    for ck in range(8):
                        MM(PS[pb][:, 0:TG], ub[s_][:, ck, :], hn2T[:, ck, :], ck == 0, ck == 7, [f"ub{s_}", hTn], [PN[pb]])
                    ACT(ge[pb][:], PS[pb][:, 0:TG], AF.Gelu, [PN[pb]], [f"ge{pb}"])
                    TT("pool", cf[pb][:], ge[pb][:], GT[:, i, :], ALU.mult, [f"ge{pb}", "GT"], [f"cf{pb}"])

                def stage2(i):
                    s_ = i % NU
                    pb = i % 2
                    for th in range(2):
                        for dh in range(2):
                            bank = 4 + th * 2 + dh
                            MM(PS[bank][:, 0:512], cf[pb][:, th * 128:(th + 1) * 128], vb[s_][:, dh * 512:(dh + 1) * 512],
                               i == 0, i == 127, [f"vb{s_}", f"cf{pb}"], [PN[bank]])

                for i in range(6):
                    loads(i)
                stage1(0)
                for i in range(128):
                    if i + 6 < 128:
                        loads(i + 6)
                    if i + 1 < 128:
                        stage1(i + 1)
                    stage2(i)
                    rgen = step(rgen, 2)
                drain(rgen)
                for tk in range(2):
                    t0 = tg0 + tk * 128
                    for half in range(2):
                        bank = 4 + tk * 2 + half
                        TT("dve", fin[tk][:, half * 512:(half + 1) * 512], PS[bank][:, 0:512],
                           fin[tk][:, half * 512:(half + 1) * 512], ALU.add, [PN[bank], f"fin{tk}"], [f"fin{tk}"])
                    DMA(f"fins{tk}", out_d[b, t0:t0 + 128, :], fin[tk][:], [f"fin{tk}"], [f"out{b}_{g}"])

        S.barrier()
        with nc.Block() as block:
            S.replay(block)
    return nc


_CACHE = {}


def kernel(x, positions, attn_norm, w_in, cq_norm, w_uq, ckv_norm, w_ukv, q_norm, k_norm,
           sb_out_norm, mla_out_norm, w_o, ffn_norm, peer_w_q, peer_sub_keys, peer_u, peer_v):
    n = 8
    f32 = lambda a: np.ascontiguousarray(np.asarray(a), dtype=np.float32)
    x = f32(x)
    positions = np.asarray(positions).astype(np.int32)
    half = 16
    invf = (1.0 / (10000.0 ** (np.arange(half, dtype=np.float32) * np.float32(2.0 / 32)))).astype(np.float32)
    shared = {
        "invf": invf.reshape(1, 16),
        "g_attn": f32(attn_norm[0]).reshape(1, -1),
        "w_in": f32(w_in[0]),
        "g_cq": f32(cq_norm[0]).reshape(1, -1),
        "w_uq": f32(w_uq[0]),
        "g_ckv": f32(ckv_norm[0]).reshape(1, -1),
        "w_ukv": f32(w_ukv[0]),
        "g_q": f32(q_norm[0]).reshape(1, -1),
        "g_k": f32(k_norm[0]).reshape(1, -1),
        "g_sbo": f32(sb_out_norm[0]).reshape(1, -1),
        "g_mlao": f32(mla_out_norm[0]).reshape(1, -1),
        "w_o": f32(w_o[0]),
        "g_ffn": f32(ffn_norm[0]).reshape(1, -1),
        "w_pq": f32(peer_w_q[0]),
        "skT": f32(np.transpose(np.asarray(peer_sub_keys[0]), (3, 1, 0, 2)).reshape(128, 16, 128)),
        "uT": f32(np.asarray(peer_u[0]).T),
        "pv": f32(peer_v[0]),
    }
    in_maps = []
    for c in range(n):
        m = dict(shared)
        m["x"] = np.ascontiguousarray(x[NSEQ * c:NSEQ * (c + 1)])
        p = positions[NSEQ * c:NSEQ * (c + 1)].reshape(NSEQ, NB, 128)
        m["pos"] = np.ascontiguousarray(np.transpose(p, (2, 0, 1)).reshape(128, NSEQ * NB))
        in_maps.append(m)
    if "nc" not in _CACHE:
        _CACHE["nc"] = build_program()
    res = run_bass_kernel_spmd(_CACHE["nc"], in_maps, core_ids=list(range(n)))
    out = np.concatenate([np.asarray(r["out"]) for r in res.results], axis=0)
    return out.astype(np.float32)
```
